# Optimizing a Trainium2 kernel written in Bass

```python
import jax, jax.numpy as jnp
from jax import lax
import numpy as np

D_MODEL = 4096
BATCH = 4
SEQ = 4096
DEPTH = 1

ATT_HEAD_DIM = 128
ATT_Q_HEADS = 32
ATT_KV_HEADS = 8
ATT_GROUP = ATT_Q_HEADS // ATT_KV_HEADS
ATT_Q_DIM = ATT_Q_HEADS * ATT_HEAD_DIM
ATT_KV_DIM = ATT_KV_HEADS * ATT_HEAD_DIM
ATT_QKV_DIM = ATT_Q_DIM + 2 * ATT_KV_DIM
WINDOW = 128
BLOCK = 128
ROPE_THETA = 10000.0
NEG_INF = -1e30

RWKV_HEAD = 64
RWKV_DIM = D_MODEL
RWKV_HEADS = RWKV_DIM // RWKV_HEAD
D_DECAY = max(32, int(round(1.8 * RWKV_DIM ** 0.5 / 32)) * 32)
D_AAA = max(32, int(round(1.8 * RWKV_DIM ** 0.5 / 32)) * 32)
D_GATE = max(32, int(round(0.6 * RWKV_DIM ** 0.8 / 32)) * 32)
RWKV_SHIFT_DIM = 3 * RWKV_DIM + D_DECAY + D_AAA + D_GATE
RWKV_SPLITS = [RWKV_DIM, 2 * RWKV_DIM, 3 * RWKV_DIM, 3 * RWKV_DIM + D_DECAY, 3 * RWKV_DIM + D_DECAY + D_AAA]
GN_EPS = 64e-5

IN_DIM = ATT_QKV_DIM + RWKV_SHIFT_DIM + 2 * D_MODEL

FFN_DIM = ((8 * D_MODEL + 3 * 256 - 1) // (3 * 256)) * 256
RMS_EPS = 1e-6

kernel_name = 'hybrid_swa_sinks_rwkv7_gated_block'


def _rmsnorm(x, g):
    xf = x.astype(jnp.float32)
    y = xf * lax.rsqrt(jnp.mean(xf * xf, axis=-1, keepdims=True) + RMS_EPS)
    return (y * g.astype(jnp.float32)).astype(x.dtype)


def _rope_tables(seq, dtype):
    pos = jnp.arange(seq, dtype=jnp.float32)
    inv_freq = ROPE_THETA ** (-jnp.arange(0, ATT_HEAD_DIM, 2, dtype=jnp.float32) / ATT_HEAD_DIM)
    ang = pos[:, None] * inv_freq[None, :]
    return jnp.cos(ang).astype(dtype), jnp.sin(ang).astype(dtype)


def _rope(t, cos, sin):
    t1, t2 = jnp.split(t, 2, axis=-1)
    c = cos[None, :, None, :]
    s = sin[None, :, None, :]
    return jnp.concatenate([t1 * c - t2 * s, t2 * c + t1 * s], axis=-1)


def _swa_sinks(q, k, v, sinks):
    B, S = q.shape[0], q.shape[1]
    nb = S // BLOCK
    qb = q.reshape(B, nb, BLOCK, ATT_KV_HEADS, ATT_GROUP, ATT_HEAD_DIM)

    def band(t):
        tb = t.reshape(B, nb, BLOCK, ATT_KV_HEADS, ATT_HEAD_DIM)
        prev = jnp.pad(tb[:, :-1], ((0, 0), (1, 0), (0, 0), (0, 0), (0, 0)))
        return jnp.concatenate([prev, tb], axis=2)

    kb, vb = band(k), band(v)
    scores = jnp.einsum('bnqhgd,bnkhd->bnhgqk', qb, kb,
                        preferred_element_type=jnp.float32) * (ATT_HEAD_DIM ** -0.5)
    qi = jnp.arange(BLOCK)[:, None]
    kj = jnp.arange(2 * BLOCK)[None, :]
    rel = qi + BLOCK - kj
    key_pos = jnp.arange(nb)[:, None, None] * BLOCK + kj[None] - BLOCK
    mask = (rel >= 0) & (rel < WINDOW) & (key_pos >= 0)
    scores = jnp.where(mask[None, :, None, None], scores, NEG_INF)
    sink = sinks.astype(jnp.float32).reshape(ATT_KV_HEADS, ATT_GROUP)[None, None, :, :, None, None]
    m = jnp.maximum(jnp.max(scores, axis=-1, keepdims=True), sink)
    p = jnp.exp(scores - m)
    probs = p / (jnp.sum(p, axis=-1, keepdims=True) + jnp.exp(sink - m))
    o = jnp.einsum('bnhgqk,bnkhd->bnqhgd', probs.astype(v.dtype), vb)
    return o.reshape(B, S, ATT_Q_DIM)


def _wkv7_step(state, inp):
    r, w, k, v, a, b = inp
    sa = jnp.einsum('bhvk,bhk->bhv', state, a)
    state = state * w[:, :, None, :] + sa[..., None] * b[:, :, None, :] + v[..., None] * k[:, :, None, :]
    y = jnp.einsum('bhvk,bhk->bhv', state, r)
    return state, y


def _rwkv7(p_r, p_k, p_v, p_w, p_a, p_g, w0, w2, a0, a2, g2, k_k, k_a, r_k, ln_w, ln_b):
    B, S, C = p_r.shape
    H, N = RWKV_HEADS, RWKV_HEAD
    f32 = jnp.float32
    heads = lambda t: t.astype(f32).reshape(B, S, H, N)
    w = -jax.nn.softplus(-(w0 + jnp.tanh(p_w) @ w2)) - 0.5
    a = jax.nn.sigmoid(a0 + p_a @ a2)
    g = jax.nn.sigmoid(p_g) @ g2
    kk = heads(p_k * k_k)
    kk = kk / jnp.maximum(jnp.sqrt(jnp.sum(kk * kk, axis=-1, keepdims=True)), 1e-12)
    k = heads(p_k * (1.0 + (a - 1.0) * k_a))
    r = heads(p_r)
    v = heads(p_v)
    decay = jnp.exp(-jnp.exp(heads(w)))
    a_vec = -kk
    b_vec = kk * heads(a)
    tm = lambda t: jnp.moveaxis(t, 1, 0)
    state0 = jnp.zeros((B, H, N, N), f32)
    _, y = lax.scan(_wkv7_step, state0, (tm(r), tm(decay), tm(k), tm(v), tm(a_vec), tm(b_vec)))
    y = jnp.moveaxis(y, 0, 1)
    mu = jnp.mean(y, axis=-1, keepdims=True)
    var = jnp.mean(jnp.square(y - mu), axis=-1, keepdims=True)
    y = ((y - mu) * lax.rsqrt(var + GN_EPS)).reshape(B, S, C)
    y = y * ln_w.astype(f32) + ln_b.astype(f32)
    bonus = (jnp.sum(r * k * r_k.astype(f32), axis=-1, keepdims=True) * v).reshape(B, S, C)
    return ((y + bonus) * g.astype(f32)).astype(p_r.dtype)


def _layer(x, norm_mix_pre, norm_mix_post, norm_ffn_pre, norm_ffn_post, w_in, b_qkv, att_sinks,
           mu_shift, w0, w2, a0, a2, g2, k_k, k_a, r_k, ln_x_w, ln_x_b,
           w_att_branch, w_rwkv_branch, w_out, w_ffn_gate, w_ffn_up, w_ffn_down):
    B, S, _ = x.shape
    h = _rmsnorm(x, norm_mix_pre)
    proj = h @ w_in
    att_cols, rwkv_cols, gate_cols = jnp.split(
        proj, [ATT_QKV_DIM, ATT_QKV_DIM + RWKV_SHIFT_DIM], axis=-1)

    att_cols = att_cols + b_qkv
    q, k, v = jnp.split(att_cols, [ATT_Q_DIM, ATT_Q_DIM + ATT_KV_DIM], axis=-1)
    q = q.reshape(B, S, ATT_Q_HEADS, ATT_HEAD_DIM)
    k = k.reshape(B, S, ATT_KV_HEADS, ATT_HEAD_DIM)
    v = v.reshape(B, S, ATT_KV_HEADS, ATT_HEAD_DIM)
    cos, sin = _rope_tables(S, q.dtype)
    o_att = _swa_sinks(_rope(q, cos, sin), _rope(k, cos, sin), v, att_sinks)

    prev = jnp.pad(rwkv_cols[:, :-1], ((0, 0), (1, 0), (0, 0)))
    rwkv_cols = rwkv_cols + (prev - rwkv_cols) * mu_shift
    p_r, p_k, p_v, p_w, p_a, p_g = jnp.split(rwkv_cols, RWKV_SPLITS, axis=-1)
    o_rwkv = _rwkv7(p_r, p_k, p_v, p_w, p_a, p_g, w0, w2, a0, a2, g2, k_k, k_a, r_k, ln_x_w, ln_x_b)

    g_att, g_rwkv = jnp.split(gate_cols, 2, axis=-1)
    merged = (jax.nn.sigmoid(g_att) * (o_att @ w_att_branch)
              + jax.nn.sigmoid(g_rwkv) * (o_rwkv @ w_rwkv_branch))
    x = x + _rmsnorm(merged @ w_out, norm_mix_post)

    h = _rmsnorm(x, norm_ffn_pre)
    f = (jax.nn.silu(h @ w_ffn_gate) * (h @ w_ffn_up)) @ w_ffn_down
    return x + _rmsnorm(f, norm_ffn_post)


def _normal(k, shape, scale):
    return jax.random.normal(k, shape, jnp.float32) * scale


def setup_inputs(seed: int = 0) -> dict:
    key = jax.random.key(seed)
    ks = jax.random.split(key, 26)
    L = DEPTH
    gain = lambda k: 1.0 + _normal(k, (L, D_MODEL), 0.02)
    return {
        'x': _normal(ks[0], (BATCH, SEQ, D_MODEL), 1.0),
        'norm_mix_pre': gain(ks[1]),
        'norm_mix_post': gain(ks[2]),
        'norm_ffn_pre': gain(ks[3]),
        'norm_ffn_post': gain(ks[4]),
        'w_in': _normal(ks[5], (L, D_MODEL, IN_DIM), D_MODEL ** -0.5),
        'b_qkv': _normal(ks[6], (L, ATT_QKV_DIM), 0.02),
        'att_sinks': _normal(ks[7], (L, ATT_Q_HEADS), 0.5),
        'mu_shift': jax.random.uniform(ks[8], (L, RWKV_SHIFT_DIM), jnp.float32, 0.0, 1.0),
        'w0': jax.random.uniform(ks[9], (L, RWKV_DIM), jnp.float32, -3.0, 0.0),
        'w2': _normal(ks[10], (L, D_DECAY, RWKV_DIM), 0.1 * D_DECAY ** -0.5),
        'a0': _normal(ks[11], (L, RWKV_DIM), 0.1),
        'a2': _normal(ks[12], (L, D_AAA, RWKV_DIM), 0.5 * D_AAA ** -0.5),
        'g2': _normal(ks[13], (L, D_GATE, RWKV_DIM), D_GATE ** -0.5),
        'k_k': 0.85 + _normal(ks[14], (L, RWKV_DIM), 0.02),
        'k_a': 1.0 + _normal(ks[15], (L, RWKV_DIM), 0.02),
        'r_k': _normal(ks[16], (L, RWKV_HEADS, RWKV_HEAD), 0.1),
        'ln_x_w': 1.0 + _normal(ks[17], (L, RWKV_DIM), 0.02),
        'ln_x_b': _normal(ks[18], (L, RWKV_DIM), 0.02),
        'w_att_branch': _normal(ks[19], (L, ATT_Q_DIM, D_MODEL), ATT_Q_DIM ** -0.5),
        'w_rwkv_branch': _normal(ks[20], (L, RWKV_DIM, D_MODEL), RWKV_DIM ** -0.5),
        'w_out': _normal(ks[21], (L, D_MODEL, D_MODEL), D_MODEL ** -0.5),
        'w_ffn_gate': _normal(ks[22], (L, D_MODEL, FFN_DIM), D_MODEL ** -0.5),
        'w_ffn_up': _normal(ks[23], (L, D_MODEL, FFN_DIM), D_MODEL ** -0.5),
        'w_ffn_down': _normal(ks[24], (L, FFN_DIM, D_MODEL), FFN_DIM ** -0.5),
    }


def reference(x, norm_mix_pre, norm_mix_post, norm_ffn_pre, norm_ffn_post, w_in, b_qkv, att_sinks,
              mu_shift, w0, w2, a0, a2, g2, k_k, k_a, r_k, ln_x_w, ln_x_b,
              w_att_branch, w_rwkv_branch, w_out, w_ffn_gate, w_ffn_up, w_ffn_down):
    for l in range(DEPTH):
        x = _layer(x, norm_mix_pre[l], norm_mix_post[l], norm_ffn_pre[l], norm_ffn_post[l],
                   w_in[l], b_qkv[l], att_sinks[l], mu_shift[l], w0[l], w2[l], a0[l], a2[l],
                   g2[l], k_k[l], k_a[l], r_k[l], ln_x_w[l], ln_x_b[l],
                   w_att_branch[l], w_rwkv_branch[l], w_out[l],
                   w_ffn_gate[l], w_ffn_up[l], w_ffn_down[l])
    return x
```

```python
import numpy as np
import ml_dtypes
from contextlib import ExitStack
import concourse.bass as bass
import concourse.mybir as mybir
from concourse.bass_utils import run_bass_kernel_spmd

F32 = mybir.dt.float32
BF16 = mybir.dt.bfloat16
AF = mybir.ActivationFunctionType
ALU = mybir.AluOpType

D = 4096
NB = 32
TT = 256
NCH = TT // 64
NQB = TT // 128
ATT_QKV = 6144
RW0 = 6144
RWS = RW0 + 12288
GATE0 = RW0 + 13024
IN_DIM = 27360
FFN = 11008
NFB = FFN // 128
KAPPA = float(np.exp(-0.5))
SCALE = float(128 ** -0.5)
NSLOT = 3
NUSCR = 321
MASKNEG = -30000.0

PV = {}
_o = 0
for _n, _c in [("nmp", 32), ("nmo", 32), ("nfp", 32), ("nfo", 32), ("bqkv", 48), ("mu", 102), ("w0", 32),
               ("a0", 32), ("kk", 32), ("ka", 32), ("lnw", 32), ("lnb", 32), ("rk", 32), ("sink", 32),
               ("omka", 32)]:
    PV[_n] = _o
    _o += _c
NPV = _o
CB_ID, CB_ONES, CB_BDONES, CB_PSW = 0, 128, 256, 384
CB_MPREV, CB_MCUR, CB_MCAT, CB_BDM = 512, 1024, 1536, 2048
CB_MPREV0 = 2048 + NCH * 128
NCBF = CB_MPREV0 + 512
CF_ID, CF_SEG = 0, 128
NCF = 128 + TT


class StopEmit(Exception):
    pass


DBG = {"stop": None, "dumps": []}


class Buf:
    __slots__ = ("w", "r", "name", "excl")

    def __init__(self, name="", excl=False):
        self.w = None
        self.r = {}
        self.name = name
        self.excl = excl


class Eng:
    def __init__(self, name, e, sem):
        self.name = name
        self.e = e
        self.sem = sem
        self.count = 0
        self.waited = {}


class KB:
    def __init__(self, nc, es, null=False):
        self.nc = nc
        self.null = null
        self.engs = {}
        if not null:
            for name, e in [("pe", nc.tensor), ("act", nc.scalar), ("dve", nc.vector), ("pool", nc.gpsimd),
                            ("sp", nc.sync)]:
                self.engs[name] = Eng(name, e, es.enter_context(nc.semaphore("s_" + name)))
            self.dsem = {}
            for q, n in [("sp", 8), ("pool", NSLOT + 3)]:
                self.dsem[q] = [[es.enter_context(nc.semaphore("d_%s%d" % (q, i))), 0] for i in range(n)]
            self.drr = {"sp": 0, "pool": 0}
            self.sp_out = {}

    def _wait(self, eng, t):
        if t is None:
            return
        key, sem, val = t
        if key == eng.name and key == "pe":
            return
        if eng.waited.get(key, 0) >= val:
            return
        eng.e.wait_ge(sem, val)
        eng.waited[key] = val

    def _deps(self, eng, reads, writes):
        for b in reads:
            self._wait(eng, b.w)
            if b.excl:
                for t in b.r.values():
                    if t[0] != eng.name:
                        self._wait(eng, t)
        for b in writes:
            self._wait(eng, b.w)
            for t in b.r.values():
                self._wait(eng, t)

    def _mark(self, t, reads, writes):
        for b in writes:
            b.w = t
            b.r = {}
        for b in reads:
            b.r[t[0]] = t

    def op(self, en, fn, reads=(), writes=()):
        if self.null:
            return None
        eng = self.engs[en]
        self._deps(eng, reads, writes)
        inst = fn(eng.e)
        eng.count += 1
        inst.then_inc(eng.sem, 1)
        t = (en, eng.sem, eng.count)
        self._mark(t, reads, writes)
        return t

    def dma(self, q, out, in_, reads=(), writes=()):
        if self.null:
            return None
        eng = self.engs[q]
        self._deps(eng, reads, writes)
        slots = self.dsem[q]
        i = self.drr[q]
        self.drr[q] = (i + 1) % len(slots)
        sem, cnt = slots[i]
        key = "d_%s%d" % (q, i)
        if cnt > 0:
            self._wait(eng, (key, sem, cnt))
        eng.e.dma_start(out=out, in_=in_).then_inc(sem, 16)
        slots[i][1] = cnt + 16
        t = (key, sem, cnt + 16)
        self._mark(t, reads, writes)
        if q == "sp":
            self.sp_out[key] = t
        return t

    def barrier(self, names=("pe", "act", "dve", "sp")):
        if self.null:
            return
        for n in names:
            eng = self.engs[n]
            for m in names:
                if m == n:
                    continue
                o = self.engs[m]
                if o.count > 0:
                    self._wait(eng, (m, o.sem, o.count))
            for t in self.sp_out.values():
                self._wait(eng, t)

    def finish(self):
        if self.null:
            return
        sp = self.engs["sp"]
        for n in ("pe", "act", "dve"):
            o = self.engs[n]
            if o.count > 0:
                self._wait(sp, (n, o.sem, o.count))
        for t in self.sp_out.values():
            self._wait(sp, t)
        for q in ("pool",):
            for i, (sem, cnt) in enumerate(self.dsem[q]):
                if cnt > 0:
                    self._wait(sp, ("d_%s%d" % (q, i), sem, cnt))
        if getattr(self, "cvsem", None) is not None:
            sem, ws_ = self.cvsem
            if ws_.cv_issued > 0:
                sp.e.wait_ge(sem, 16 * ws_.cv_issued)


class WStream:
    def __init__(self):
        self.plan = []
        self.record = True
        self.used = 0
        self.issued = 0

    def start_emit(self, kb, slots, T_, npre=0, cvsem=None):
        self.record = False
        self.muted = False
        self.kb = kb
        self.T_ = T_
        self.npre = npre
        self.cvsem = cvsem
        self.cv_issued = 0
        self.cv_waited = False
        self.cv_plan = []
        self.sidx = {}
        self.slots = slots
        self.used = 0
        self.issued = 0
        if npre > 0:
            for key, segs in self.plan:
                if key[1] != npre or key[0] == "lora":
                    continue
                idx = len(self.sidx)
                self.sidx[(key[0],) + tuple(key[2:])] = idx
                for sg_ in segs:
                    nkc = sg_[7]
                    h = (nkc + 1) // 2
                    self.cv_plan.append((idx, sg_, 0, h))
                    if nkc > h:
                        self.cv_plan.append((idx, sg_, h, nkc))

    def _scr(self, idx):
        per = NUSCR // 3
        assert idx < NUSCR
        return self.T_["wsc%d" % (idx // per)][idx % per]

    def _convert(self, n):
        eng = self.kb.engs["pool"]
        while n > 0 and self.cv_issued < len(self.cv_plan):
            j = self.cv_issued
            idx, (W, r0, nr, c0, ncols, prow, kc0, nkc, cdst), kA, kB = self.cv_plan[j]
            if j >= 1:
                eng.e.wait_ge(self.cvsem, 16 * j)
            src = self.T_[W][r0 + kA * 128:r0 + kB * 128, c0:c0 + ncols].rearrange("(k p) c -> p k c", p=128)
            dst = self._scr(idx).rearrange("p (k c) -> p k c", c=256)[:, kc0 + kA:kc0 + kB, cdst:cdst + ncols]
            eng.e.dma_start(out=dst, in_=src).then_inc(self.cvsem, 16)
            self.cv_issued += 1
            n -= 1

    def get(self, key, segs):
        if getattr(self, "muted", False):
            return (self.slots[0] if not self.record else (None, None))
        if self.record:
            self.plan.append((key, segs))
            return None, None
        u = self.used
        assert self.plan[u][0] == key, (key, self.plan[u][0])
        while self.issued < min(len(self.plan), u + NSLOT):
            self._issue(self.issued)
            self.issued += 1
        self.used += 1
        return self.slots[u % NSLOT]

    def _issue(self, i):
        key, segs = self.plan[i]
        tile, buf = self.slots[i % NSLOT]
        use_bf = self.npre > 0 and key[1] >= self.npre
        if use_bf and not self.cv_waited:
            self._convert(len(self.cv_plan))
            self.kb.engs["pool"].e.wait_ge(self.cvsem, 16 * len(self.cv_plan))
            self.cv_waited = True
        skey = (key[0],) + tuple(key[2:])
        if use_bf and skey in self.sidx:
            kcm = max(sg_[6] + sg_[7] for sg_ in segs)
            src = self._scr(self.sidx[skey])[:, 0:kcm * 256].rearrange("p (k c) -> p k c", c=256)
            self.kb.dma("pool", tile[:, 0:kcm, :], src, writes=[buf])
        else:
            for (W, r0, nr, c0, ncols, prow, kc0, nkc, cdst) in segs:
                src = self.T_[W][r0:r0 + nr, c0:c0 + ncols].rearrange("(k p) c -> p k c", p=prow)
                self.kb.dma("pool", tile[0:prow, kc0:kc0 + nkc, cdst:cdst + ncols], src, writes=[buf])
        if self.npre > 0 and not use_bf:
            self._convert(2)


def wsegs(W, r0, nrows, c0, ncols, kc0=0, cdst=0):
    segs = []
    nfull = nrows // 128
    if nfull:
        segs.append((W, r0, nfull * 128, c0, ncols, 128, kc0, nfull, cdst))
    rem = nrows - nfull * 128
    if rem:
        segs.append((W, r0 + nfull * 128, rem, c0, ncols, rem, kc0 + nfull, 1, cdst))
    return segs


def emit_program(nc, kb, ws, es, T_, NPRE, NT):
    null = kb.null

    def mute(flag):
        kb.null = True if null else flag
        ws.muted = flag

    uniq = [0]

    def sbt(st, name, shape, dt):
        uniq[0] += 1
        return st.enter_context(nc.sbuf_tensor("%s_%d" % (name, uniq[0]), shape, dt))

    x, out = T_["x"], T_["out"]
    w_in = T_["w_in"]

    pv = sbt(es, "pv", [128, NPV], F32)
    cb = sbt(es, "cb", [128, NCBF], BF16)
    cf = sbt(es, "cf", [128, NCF], F32)
    B_const = Buf("const")
    wslots = []
    for i in range(NSLOT):
        wslots.append((sbt(es, "wslot%d" % i, [128, 32, 256], BF16), Buf("w%d" % i)))
    if not null:
        cvsem = es.enter_context(nc.semaphore("cvsem"))
        ws.start_emit(kb, wslots, T_, NPRE, cvsem)
        kb.cvsem = (cvsem, ws)
    st32 = sbt(es, "st32", [128, 32, 128], F32)
    B_st = [Buf("st%d" % p) for p in range(32)]
    carry = sbt(es, "carry", [128, 104], F32)
    B_carry = Buf("carry")
    kT = sbt(es, "kT", [128, 8, 128 + TT], BF16)
    B_kT = Buf("kT")
    vtok = sbt(es, "vtok", [128, 1 + NQB, 8, 128], BF16)
    B_vtok = Buf("vtok")
    kmx = sbt(es, "kmx", [128, 1 + NQB, 8], F32)
    B_kmx = Buf("kmx")
    pbank = [es.enter_context(nc.psum_tensor("pb%d" % i, [128, 512], F32)) for i in range(7)]
    ptb = es.enter_context(nc.psum_tensor("ptb", [128, 1024], BF16))
    B_pb = [Buf("pb%d" % i, excl=True) for i in range(7)]
    _bpt = Buf("pt", excl=True)
    B_pt = [_bpt, _bpt]
    ACC = [0, 1]
    YB = 2
    ROT = [3, 4, 5, 6]
    state = {"acc": 0, "rot": 0, "pt": 0, "ew": 0}

    def next_acc():
        i = ACC[state["acc"] % 2]
        state["acc"] += 1
        return pbank[i], B_pb[i]

    def next_rot():
        i = ROT[state["rot"] % len(ROT)]
        state["rot"] += 1
        return pbank[i], B_pb[i]

    def next_pt():
        i = state["pt"] % 2
        state["pt"] += 1
        return ptb[:, i * 512:(i + 1) * 512], B_pt[i]

    def ew():
        state["ew"] += 1
        return "dve" if state["ew"] % 2 else "act"

    ident_f = cf[:, CF_ID:CF_ID + 128]
    segmask = cf[:, CF_SEG:CF_SEG + TT]
    ident_b = cb[:, CB_ID:CB_ID + 128]
    ones_b = cb[:, CB_ONES:CB_ONES + 128]
    bdones_b = cb[:, CB_BDONES:CB_BDONES + 128]
    psw_b = cb[:, CB_PSW:CB_PSW + 128]
    mprev_b = cb[:, CB_MPREV:CB_MPREV + 512]
    mcur_b = cb[:, CB_MCUR:CB_MCUR + 512]
    mcat_b = cb[:, CB_MCAT:CB_MCAT + 512]
    bdm_b = cb[:, CB_BDM:CB_BDM + NCH * 128].rearrange("p (c h t) -> p c h t", c=NCH, h=2)

    def pcol(name, j):
        o = PV[name] + j
        return pv[:, o:o + 1]

    def ckpt(label, dumps=()):
        if label in DBG.get("dump_at", ()) and not kb.null:
            for (name, ap, buf, dt) in dumps:
                dd = nc.dram_tensor("dbg_" + name, list(ap.shape), dt, kind="ExternalOutput").ap()
                kb.dma("sp", dd, ap, reads=[buf])
            return
        if DBG["stop"] != label:
            return
        if not null:
            for (name, ap, buf, dt) in dumps:
                shp = list(ap.shape)
                dd = nc.dram_tensor("dbg_" + name, shp, dt, kind="ExternalOutput").ap()
                kb.dma("sp", dd, ap, reads=[buf])
                DBG["dumps"].append("dbg_" + name)
        kb.null = True
        ws.muted = True

    kb.dma("sp", pv[:, :], T_["pvec"][:, :], writes=[B_const])
    kb.dma("sp", cb[:, :], T_["cbf"][:, :], writes=[B_const])
    kb.dma("sp", cf[:, :], T_["cf32"][:, :], writes=[B_const])
    kb.op("dve", lambda e: e.tensor_scalar(out=pv[:, PV["omka"]:PV["omka"] + 32], in0=pv[:, PV["ka"]:PV["ka"] + 32],
                                           scalar1=-1.0, scalar2=1.0, op0=ALU.mult, op1=ALU.add),
          reads=[B_const], writes=[B_const])
    kb.op("dve", lambda e: e.memset(st32[:, :, :], 0.0), writes=B_st)
    kb.op("dve", lambda e: e.memset(carry[:, :], 0.0), writes=[B_carry])
    kb.op("dve", lambda e: e.memset(kT[:, :, :], 0.0), writes=[B_kT])
    kb.op("dve", lambda e: e.memset(vtok[:, :, :, :], 0.0), writes=[B_vtok])
    kb.op("dve", lambda e: e.memset(kmx[:, :, :], 0.0), writes=[B_kmx])
    kb.barrier()

    def dense(acc, B_acc, parts):
        n = sum(len(p[2]) for p in parts)

        def fn(e):
            i = 0
            inst = None
            for (wt, bw, kcs, c0, rhs_fn, rb, prows) in parts:
                for j, kc in enumerate(kcs):
                    pr = prows[j] if prows is not None else 128
                    inst = e.matmul(acc[:, 0:TT], lhsT=wt[0:pr, kc, c0:c0 + 128], rhs=rhs_fn(j, pr),
                                    start=(i == 0), stop=(i == n - 1))
                    i += 1
            return inst
        reads = []
        for p in parts:
            reads.append(p[1])
            reads.extend(p[5])
        kb.op("pe", fn, reads=reads, writes=[B_acc])

    mprev0_b = cb[:, CB_MPREV0:CB_MPREV0 + 512]
    for t in range(NPRE + NT):
        tok0 = t * TT
        otok0 = max(0, t - NPRE) * TT
        is_main = t >= NPRE
        lastpre = (t == NPRE - 1)
        mute(False)
        stile = ExitStack()
        DBG["stile"] = stile
        hT = sbt(stile, "hT", [128, NB, TT], BF16)
        B_hT = Buf("hT")
        with ExitStack() as sm:
            orw = sbt(sm, "orw", [128, NB, TT], BF16)
            B_orw = Buf("orw")
            h_rhs = (lambda j, pr: None)

            with ExitStack() as s0:
                xt = sbt(s0, "xt", [128, D], F32)
                B_xt = Buf("xt")
                ssq = sbt(s0, "ssq", [128, 2], F32)
                B_ssq = Buf("ssq")
                dg = sbt(s0, "dg", [128, 128], F32)
                B_dg = Buf("dg")
                for tb in range(NQB):
                    r0 = tok0 + tb * 128
                    kb.dma("sp", xt[:, :], x[r0:r0 + 128, :], writes=[B_xt])
                    kb.op("act", lambda e: e.activation(out=orw[:, 0:16, :].rearrange("p a b -> p (a b)"),
                                                        in_=xt[:, :], func=AF.Square, accum_out=ssq[:, 0:1]),
                          reads=[B_xt], writes=[B_orw, B_ssq])
                    kb.op("dve", lambda e: e.tensor_scalar(out=ssq[:, 1:2], in0=ssq[:, 0:1], scalar1=1.0 / D,
                                                           scalar2=1e-6, op0=ALU.mult, op1=ALU.add),
                          reads=[B_ssq], writes=[B_ssq])
                    kb.op("act", lambda e: e.activation(out=ssq[:, 1:2], in_=ssq[:, 1:2], func=AF.Sqrt),
                          reads=[B_ssq], writes=[B_ssq])
                    kb.op("dve", lambda e: e.reciprocal(out=ssq[:, 1:2], in_=ssq[:, 1:2]), reads=[B_ssq],
                          writes=[B_ssq])
                    kb.op("dve", lambda e: e.tensor_scalar_mul(out=dg[:, :], in0=ident_f, scalar1=ssq[:, 1:2]),
                          reads=[B_ssq, B_const], writes=[B_dg])
                    for k4 in range(NB // 4):
                        pb, Bp = next_rot()

                        def fn(e, k4=k4, pb=pb):
                            inst = None
                            for i in range(4):
                                kc = k4 * 4 + i
                                inst = e.matmul(pb[:, i * 128:(i + 1) * 128], lhsT=xt[:, kc * 128:(kc + 1) * 128],
                                                rhs=dg[:, :], start=True, stop=True)
                            return inst
                        kb.op("pe", fn, reads=[B_xt, B_dg], writes=[Bp])
                        for i in range(4):
                            kc = k4 * 4 + i
                            en = ew()
                            if en == "act":
                                kb.op("act", lambda e, kc=kc, i=i, pb=pb, tb=tb: e.activation(
                                    out=hT[:, kc, tb * 128:(tb + 1) * 128], in_=pb[:, i * 128:(i + 1) * 128],
                                    func=AF.Identity, scale=pcol("nmp", kc)), reads=[Bp, B_const], writes=[B_hT])
                            else:
                                kb.op("dve", lambda e, kc=kc, i=i, pb=pb, tb=tb: e.tensor_scalar_mul(
                                    out=hT[:, kc, tb * 128:(tb + 1) * 128], in0=pb[:, i * 128:(i + 1) * 128],
                                    scalar1=pcol("nmp", kc)), reads=[Bp, B_const], writes=[B_hT])
                kb.barrier()
            ckpt("S0", [("hT", hT[:, :, :].rearrange("p a b -> p (a b)"), B_hT, BF16)])

            def hT_rhs(j, pr, kcs=None):
                return hT[:, j, :]

            def proj_in(c0, ncols, key):
                return ws.get(key, wsegs("w_in", 0, D, c0, ncols))

            def dense_h(wt, bw, cofs):
                acc, Ba = next_acc()
                if not null:
                    dense(acc, Ba, [(wt, bw, list(range(NB)), cofs, lambda j, pr: hT[:, j, :], [B_hT], None)])
                return acc, Ba

            with ExitStack() as sr:
                tw = sbt(sr, "tw", [128, TT], BF16)
                pa = sbt(sr, "pa", [128, TT], BF16)
                sg = sbt(sr, "sg", [128, 4, TT], BF16)
                B_small = Buf("small")
                raw = sbt(sr, "raw", [128, TT + 1], F32)
                B_raw = Buf("raw")
                dtmp = sbt(sr, "dtmp", [128, TT], F32)
                B_d = Buf("dtmp")
                XSETS = [(sbt(sr, "xs", [128, 6, TT], F32), [Buf("xs%d" % i) for i in range(6)]) for _ in range(2)]
                sm_xs = sbt(sr, "smxs", [128, TT], F32)
                B_smxs = Buf("smxs")

                def shift_block(acc, Ba, blk, dst, B_dst):
                    kb.op("act", lambda e: e.activation(out=raw[:, 1:TT + 1], in_=acc[:, 0:TT], func=AF.Copy),
                          reads=[Ba], writes=[B_raw])
                    kb.op("dve", lambda e: e.tensor_copy(out=raw[:, 0:1], in_=carry[:, blk:blk + 1]),
                          reads=[B_carry], writes=[B_raw])
                    kb.op("dve", lambda e: e.tensor_copy(out=carry[:, blk:blk + 1], in_=raw[:, TT:TT + 1]),
                          reads=[B_raw], writes=[B_carry])
                    kb.op("dve", lambda e: e.tensor_tensor(out=dtmp[:, :], in0=raw[:, 0:TT], in1=raw[:, 1:TT + 1],
                                                           op=ALU.subtract), reads=[B_raw], writes=[B_d])
                    kb.op("dve", lambda e: e.scalar_tensor_tensor(out=dst, in0=dtmp[:, :], scalar=pcol("mu", blk),
                                                                  in1=raw[:, 1:TT + 1], op0=ALU.mult, op1=ALU.add),
                          reads=[B_d, B_raw, B_const], writes=[B_dst])

                for ui, (c0, ncols) in enumerate([(RWS, 256), (RWS + 256, 256), (RWS + 512, 224)]):
                    wt, bw = proj_in(c0, ncols, ("small", t, ui))
                    for bi in range(2):
                        blk_small = ui * 2 + bi
                        rows = 128 if blk_small < 5 else 96
                        acc, Ba = dense_h(wt, bw, bi * 128)
                        shift_block(acc, Ba, 96 + blk_small, sm_xs[:, :], B_smxs)
                        if blk_small == 0:
                            kb.op("act", lambda e: e.activation(out=tw[:, :], in_=sm_xs[:, :], func=AF.Tanh),
                                  reads=[B_smxs], writes=[B_small])
                        elif blk_small == 1:
                            kb.op("act", lambda e: e.activation(out=pa[:, :], in_=sm_xs[:, :], func=AF.Copy),
                                  reads=[B_smxs], writes=[B_small])
                        else:
                            kb.op("act", lambda e, g=blk_small - 2: e.activation(out=sg[:, g, :], in_=sm_xs[:, :],
                                                                                 func=AF.Sigmoid),
                                  reads=[B_smxs], writes=[B_small])

                ckpt("S1", [("tw", tw[:, :], B_small, BF16), ("pa", pa[:, :], B_small, BF16),
                            ("sg", sg[:, :, :].rearrange("p a b -> p (a b)"), B_small, BF16)])
                def mkset():
                    lw = sbt(sr, "lw", [128, TT], F32)
                    av = sbt(sr, "av", [128, TT], F32)
                    gv = sbt(sr, "gv", [128, TT], F32)
                    kk = sbt(sr, "kk", [128, TT], F32)
                    k2 = sbt(sr, "k2", [128, TT], F32)
                    bv = sbt(sr, "bv", [128, TT], F32)
                    cs = sbt(sr, "cs", [128, TT], F32)
                    csx = sbt(sr, "csx", [128, TT], F32)
                    ex = sbt(sr, "ex", [128, TT], F32)
                    exb = sbt(sr, "exb", [128, NCH, 2, 64], F32)
                    t16 = sbt(sr, "t16", [128, TT], BF16)
                    bonus = sbt(sr, "bonus", [128, TT], F32)
                    nbias = sbt(sr, "nbias", [128, 2 * NCH], F32)
                    bdA = sbt(sr, "bdA", [128, NCH, 2, 64], BF16)
                    bdB = sbt(sr, "bdB", [128, NCH, 2, 64], BF16)
                    bdK = sbt(sr, "bdK", [128, NCH, 2, 64], BF16)
                    bdBc = sbt(sr, "bdBc", [128, NCH, 2, 64], BF16)
                    bdKc = sbt(sr, "bdKc", [128, NCH, 2, 64], BF16)
                    bdV = sbt(sr, "bdV", [128, NCH, 2, 64], BF16)
                    Rt = sbt(sr, "Rt", [128, TT], BF16)
                    Bp_ = {n: Buf(n) for n in ["lw", "av", "gv", "kk", "k2", "bv", "cs", "csx", "ex", "exb", "t16",
                                                "bonus", "nbias", "bdA", "bdB", "bdK", "bdBc", "bdKc", "bdV", "Rt"]}
                    m1s = [sbt(sr, "m1", [128, 512], BF16) for _ in range(NCH)]
                    B_m1s = [Buf("m1") for _ in range(NCH)]
                    pqs = [[sbt(sr, "pq", [128, 384], BF16) for i in range(2)] for _ in range(NCH)]
                    B_pqs = [[Buf("pq0"), Buf("pq1")] for _ in range(NCH)]
                    tms = [sbt(sr, "tm", [128, 384], BF16) for _ in range(2)]
                    B_tms = [Buf("tm0"), Buf("tm1")]
                    xu = sbt(sr, "xu", [128, 256], BF16)
                    B_xu = Buf("xu")
                    T16 = sbt(sr, "T16", [128, 128], BF16)
                    B_T16 = Buf("T16")
                    yT, B_yT = cs, Bp_["cs"]
                    pt1, B_pt1 = ex, Bp_["ex"]
                    pt2, B_pt2 = csx, Bp_["csx"]
                    return dict(locals())
                SETS = [mkset(), mkset()]

                def bcast_bd(ap):
                    return ap.rearrange("p (c t) -> p c t", t=64).unsqueeze(2).broadcast_to([128, NCH, 2, 64])

                def pair_body(sp_, bi, S, XS, wl, bwl, full):
                    xs, B_xs = XS
                    (lw, av, gv, kk, k2, bv, cs, csx, ex, exb, t16, bonus, nbias, bdA, bdB, bdK, bdBc, bdKc, bdV, Rt, Bp_, m1s, B_m1s, pqs, B_pqs, tms, B_tms, xu, B_xu, T16, B_T16, yT, B_yT, pt1, B_pt1, pt2, B_pt2) = [S[k_] for k_ in ['lw', 'av', 'gv', 'kk', 'k2', 'bv', 'cs', 'csx', 'ex', 'exb', 't16', 'bonus', 'nbias', 'bdA', 'bdB', 'bdK', 'bdBc', 'bdKc', 'bdV', 'Rt', 'Bp_', 'm1s', 'B_m1s', 'pqs', 'B_pqs', 'tms', 'B_tms', 'xu', 'B_xu', 'T16', 'B_T16', 'yT', 'B_yT', 'pt1', 'B_pt1', 'pt2', 'B_pt2']]
                    p = sp_ * 2 + bi
                    rs_, ks_, vs_ = xs[:, bi, :], xs[:, 2 + bi, :], xs[:, 4 + bi, :]
                    Brs, Bks, Bvs = B_xs[bi], B_xs[2 + bi], B_xs[4 + bi]
                    c0 = bi * 128
                    pb, Bp = next_rot()
                    yield kb.op("pe", lambda e, pb=pb, c0=c0: e.matmul(pb[:, 0:TT], lhsT=wl[:, 0, c0:c0 + 128],
                                                                 rhs=tw[:, :], start=True, stop=True),
                          reads=[bwl, B_small], writes=[Bp])
                    yield kb.op("act", lambda e, pb=pb, p=p: e.activation(out=lw[:, :], in_=pb[:, 0:TT], func=AF.Sigmoid,
                                                                    bias=pcol("w0", p)),
                          reads=[Bp, B_const], writes=[Bp_["lw"]])
                    pb, Bp = next_rot()
                    yield kb.op("pe", lambda e, pb=pb, c0=c0: e.matmul(pb[:, 0:TT], lhsT=wl[:, 1, c0:c0 + 128],
                                                                 rhs=pa[:, :], start=True, stop=True),
                          reads=[bwl, B_small], writes=[Bp])
                    yield kb.op("act", lambda e, pb=pb, p=p: e.activation(out=av[:, :], in_=pb[:, 0:TT], func=AF.Sigmoid,
                                                                    bias=pcol("a0", p)),
                          reads=[Bp, B_const], writes=[Bp_["av"]])
                    if full:
                        pb, Bp = next_rot()

                        def fng(e, pb=pb, c0=c0):
                            inst = None
                            for g in range(4):
                                pr = 128 if g < 3 else 96
                                inst = e.matmul(pb[:, 0:TT], lhsT=wl[0:pr, 2 + g, c0:c0 + 128], rhs=sg[0:pr, g, :],
                                                start=(g == 0), stop=(g == 3))
                            return inst
                        yield kb.op("pe", fng, reads=[bwl, B_small], writes=[Bp])
                        yield kb.op("act", lambda e, pb=pb: e.activation(out=gv[:, :], in_=pb[:, 0:TT], func=AF.Copy),
                                    reads=[Bp], writes=[Bp_["gv"]])
                    else:
                        yield None
                        yield None
                    yield kb.op("dve", lambda e, p=p, ks_=ks_: e.tensor_scalar_mul(out=kk[:, :], in0=ks_, scalar1=pcol("kk", p)),
                          reads=[Bks, B_const], writes=[Bp_["kk"]])
                    yield kb.op("act", lambda e: e.activation(out=t16[:, :], in_=kk[:, :], func=AF.Square),
                          reads=[Bp_["kk"]], writes=[Bp_["t16"]])
                    pb, Bp = next_rot()
                    yield kb.op("pe", lambda e, pb=pb: e.matmul(pb[:, 0:TT], lhsT=bdones_b, rhs=t16[:, :], start=True,
                                                          stop=True), reads=[Bp_["t16"], B_const], writes=[Bp])
                    yield kb.op("dve", lambda e, pb=pb: e.tensor_scalar_max(out=ex[:, :], in0=pb[:, 0:TT], scalar1=1e-24),
                          reads=[Bp], writes=[Bp_["ex"]])
                    yield kb.op("act", lambda e: e.activation(out=ex[:, :], in_=ex[:, :], func=AF.Sqrt),
                          reads=[Bp_["ex"]], writes=[Bp_["ex"]])
                    yield kb.op("dve", lambda e: e.reciprocal(out=ex[:, :], in_=ex[:, :]), reads=[Bp_["ex"]],
                          writes=[Bp_["ex"]])
                    yield kb.op("dve", lambda e: e.tensor_tensor(out=kk[:, :], in0=kk[:, :], in1=ex[:, :], op=ALU.mult),
                          reads=[Bp_["ex"], Bp_["kk"]], writes=[Bp_["kk"]])
                    yield kb.op("dve", lambda e, p=p: e.tensor_scalar(out=k2[:, :], in0=av[:, :], scalar1=pcol("ka", p),
                                                                scalar2=pcol("omka", p), op0=ALU.mult, op1=ALU.add),
                          reads=[Bp_["av"], B_const], writes=[Bp_["k2"]])
                    yield kb.op("dve", lambda e, ks_=ks_: e.tensor_tensor(out=k2[:, :], in0=k2[:, :], in1=ks_, op=ALU.mult),
                          reads=[Bks, Bp_["k2"]], writes=[Bp_["k2"]])
                    yield kb.op("dve", lambda e: e.tensor_tensor(out=bv[:, :], in0=kk[:, :], in1=av[:, :], op=ALU.mult),
                          reads=[Bp_["kk"], Bp_["av"]], writes=[Bp_["bv"]])
                    if full:
                        yield kb.op("dve", lambda e, rs_=rs_: e.tensor_tensor(out=ex[:, :], in0=rs_, in1=k2[:, :], op=ALU.mult),
                                    reads=[Brs, Bp_["k2"]], writes=[Bp_["ex"]])
                        yield kb.op("act", lambda e, p=p: e.activation(out=t16[:, :], in_=ex[:, :], func=AF.Identity,
                                                                       scale=pcol("rk", p)),
                                    reads=[Bp_["ex"], B_const], writes=[Bp_["t16"]])
                        pb, Bp = next_rot()
                        yield kb.op("pe", lambda e, pb=pb: e.matmul(pb[:, 0:TT], lhsT=bdones_b, rhs=t16[:, :], start=True,
                                                                    stop=True), reads=[Bp_["t16"], B_const], writes=[Bp])
                        yield kb.op("dve", lambda e, pb=pb, vs_=vs_: e.tensor_tensor(out=bonus[:, :], in0=pb[:, 0:TT], in1=vs_,
                                                                                     op=ALU.mult),
                                    reads=[Bp, Bvs], writes=[Bp_["bonus"]])
                    yield kb.op("dve", lambda e: e.tensor_tensor_scan(out=cs[:, :], data0=segmask, data1=lw[:, :],
                                                                initial=0.0, op0=ALU.mult, op1=ALU.add),
                          reads=[Bp_["lw"], B_const], writes=[Bp_["cs"]])
                    yield kb.op("dve", lambda e: e.tensor_tensor(out=csx[:, :], in0=cs[:, :], in1=lw[:, :], op=ALU.subtract),
                          reads=[Bp_["cs"], Bp_["lw"]], writes=[Bp_["csx"]])
                    csC = cs[:, :].rearrange("p (c t) -> p c t", t=64)[:, :, 63:64].rearrange("p c o -> p (c o)")
                    yield kb.op("dve", lambda e: e.tensor_scalar_mul(out=nbias[:, 0:NCH], in0=csC, scalar1=-KAPPA),
                          reads=[Bp_["cs"]], writes=[Bp_["nbias"]])
                    yield kb.op("act", lambda e: e.activation(out=nbias[:, NCH:2 * NCH], in_=nbias[:, 0:NCH], func=AF.Exp),
                          reads=[Bp_["nbias"]], writes=[Bp_["nbias"]])
                    if full:
                        yield kb.op("act", lambda e: e.activation(out=ex[:, :], in_=cs[:, :], func=AF.Exp, scale=-KAPPA),
                                    reads=[Bp_["cs"]], writes=[Bp_["ex"]])
                        yield kb.op("dve", lambda e, rs_=rs_: e.tensor_tensor(out=Rt[:, :], in0=rs_, in1=ex[:, :], op=ALU.mult),
                                    reads=[Brs, Bp_["ex"]], writes=[Bp_["Rt"]])
                    yield kb.op("act", lambda e: e.activation(out=ex[:, :], in_=csx[:, :], func=AF.Exp, scale=-KAPPA),
                          reads=[Bp_["csx"]], writes=[Bp_["ex"]])
                    yield kb.op("dve", lambda e: e.scalar_tensor_tensor(out=ex[:, :], in0=kk[:, :], scalar=-1.0,
                                                                  in1=ex[:, :], op0=ALU.mult, op1=ALU.mult),
                          reads=[Bp_["kk"], Bp_["ex"]], writes=[Bp_["ex"]])
                    yield kb.op("dve", lambda e: e.tensor_tensor(out=bdA[:, :, :, :], in0=bcast_bd(ex[:, :]), in1=bdm_b,
                                                           op=ALU.mult),
                          reads=[Bp_["ex"], B_const], writes=[Bp_["bdA"]])
                    yield kb.op("act", lambda e: e.activation(out=ex[:, :], in_=cs[:, :], func=AF.Exp, scale=KAPPA),
                          reads=[Bp_["cs"]], writes=[Bp_["ex"]])
                    yield kb.op("dve", lambda e: e.tensor_tensor(out=exb[:, :, :, :], in0=bcast_bd(ex[:, :]), in1=bdm_b,
                                                           op=ALU.mult),
                          reads=[Bp_["ex"], B_const], writes=[Bp_["exb"]])
                    yield kb.op("dve", lambda e: e.tensor_tensor(out=bdB[:, :, :, :], in0=bcast_bd(bv[:, :]),
                                                           in1=exb[:, :, :, :], op=ALU.mult),
                          reads=[Bp_["bv"], Bp_["exb"]], writes=[Bp_["bdB"]])
                    yield kb.op("dve", lambda e: e.tensor_tensor(out=bdK[:, :, :, :], in0=bcast_bd(k2[:, :]),
                                                           in1=exb[:, :, :, :], op=ALU.mult),
                          reads=[Bp_["k2"], Bp_["exb"]], writes=[Bp_["bdK"]])
                    for c in range(NCH):
                        yield kb.op("act", lambda e, c=c: e.activation(out=ex[:, c * 64:(c + 1) * 64],
                                                                 in_=cs[:, c * 64:(c + 1) * 64], func=AF.Exp,
                                                                 scale=KAPPA, bias=nbias[:, c:c + 1]),
                              reads=[Bp_["cs"], Bp_["nbias"]], writes=[Bp_["ex"]])
                    yield kb.op("dve", lambda e: e.tensor_tensor(out=exb[:, :, :, :], in0=bcast_bd(ex[:, :]), in1=bdm_b,
                                                           op=ALU.mult),
                          reads=[Bp_["ex"], B_const], writes=[Bp_["exb"]])
                    yield kb.op("dve", lambda e: e.tensor_tensor(out=bdBc[:, :, :, :], in0=bcast_bd(bv[:, :]),
                                                           in1=exb[:, :, :, :], op=ALU.mult),
                          reads=[Bp_["bv"], Bp_["exb"]], writes=[Bp_["bdBc"]])
                    yield kb.op("dve", lambda e: e.tensor_tensor(out=bdKc[:, :, :, :], in0=bcast_bd(k2[:, :]),
                                                           in1=exb[:, :, :, :], op=ALU.mult),
                          reads=[Bp_["k2"], Bp_["exb"]], writes=[Bp_["bdKc"]])
                    yield kb.op("dve", lambda e, vs_=vs_: e.tensor_tensor(out=bdV[:, :, :, :], in0=bcast_bd(vs_), in1=bdm_b,
                                                                    op=ALU.mult),
                          reads=[Bvs, B_const], writes=[Bp_["bdV"]])
                    yield kb.op("act", lambda e, p=p: e.activation(out=T16[:, :], in_=st32[:, p, :], func=AF.Copy),
                          reads=[B_st[p]], writes=[B_T16])
                    yb, Byb = pbank[YB], B_pb[YB]

                    def bdc(tl, c):
                        return tl[:, c, :, :].rearrange("p h t -> p (h t)")
                    for c in range(NCH):
                        At_c, Bt_c, Kt_c = bdc(bdA, c), bdc(bdB, c), bdc(bdK, c)
                        Rt_c = Rt[:, c * 64:(c + 1) * 64]
                        pb, Bp = next_rot()

                        def fn1(e, pb=pb, At_c=At_c, Bt_c=Bt_c, Kt_c=Kt_c, Rt_c=Rt_c):
                            e.matmul(pb[:, 0:128], lhsT=Bt_c, rhs=At_c, start=True, stop=True)
                            e.matmul(pb[:, 128:256], lhsT=Kt_c, rhs=At_c, start=True, stop=True)
                            inst = e.matmul(pb[:, 256:384], lhsT=At_c, rhs=Bt_c, start=True, stop=True)
                            if not full:
                                return inst
                            e.matmul(pb[:, 384:448], lhsT=Bt_c, rhs=Rt_c, start=True, stop=True)
                            return e.matmul(pb[:, 448:512], lhsT=Kt_c, rhs=Rt_c, start=True, stop=True)
                        yield kb.op("pe", fn1, reads=[Bp_["bdA"], Bp_["bdB"], Bp_["bdK"]] + ([Bp_["Rt"]] if full else []),
                                    writes=[Bp])
                        mhi = 512 if full else 384
                        yield kb.op("dve", lambda e, pb=pb, mhi=mhi, c=c: e.tensor_tensor(
                            out=m1s[c][:, 0:mhi], in0=pb[:, 0:mhi], in1=mcat_b[:, 0:mhi], op=ALU.mult),
                            reads=[Bp, B_const], writes=[B_m1s[c]])
                    cur = [(m1s[c][:, 256:384], m1s[c][:, 0:128], ident_b, [B_m1s[c], B_const]) for c in range(NCH)]
                    for lvl in range(6):
                        for c in range(NCH):
                            curP, curQ, curM, curB = cur[c]
                            pb, Bp = next_rot()
                            dst = pqs[c][lvl % 2]
                            Bdst = B_pqs[c][lvl % 2]

                            def fnl(e, pb=pb, curP=curP, curQ=curQ, curM=curM, lvl=lvl):
                                if lvl < 5:
                                    e.matmul(pb[:, 0:128], lhsT=curQ, rhs=curP, start=True, stop=True)
                                    e.matmul(pb[:, 128:256], lhsT=curP, rhs=curQ, start=True, stop=True)
                                e.matmul(pb[:, 256:384], lhsT=curP, rhs=curM, start=True, stop=False)
                                return e.matmul(pb[:, 256:384], lhsT=ident_b, rhs=curM, start=False, stop=True)
                            yield kb.op("pe", fnl, reads=curB, writes=[Bp])
                            lo = 0 if lvl < 5 else 256
                            if (lvl + c) % 2:
                                yield kb.op("act", lambda e, pb=pb, dst=dst, lo=lo: e.activation(
                                    out=dst[:, lo:384], in_=pb[:, lo:384], func=AF.Copy), reads=[Bp], writes=[Bdst])
                            else:
                                yield kb.op("dve", lambda e, pb=pb, dst=dst, lo=lo: e.tensor_copy(
                                    out=dst[:, lo:384], in_=pb[:, lo:384]), reads=[Bp], writes=[Bdst])
                            cur[c] = (dst[:, 0:128], dst[:, 128:256], dst[:, 256:384], [Bdst, B_const])

                    def transposes(c):
                        ptile, Bpt = next_pt()
                        tm, B_tm = tms[c % 2], B_tms[c % 2]

                        def fnt(e, ptile=ptile, c=c):
                            inst = None
                            for i, src in enumerate([bdBc, bdKc, bdV]):
                                inst = e.transpose(ptile[:, i * 128:(i + 1) * 128], bdc(src, c), ident_b)
                            return inst
                        t1_ = kb.op("pe", fnt, reads=[Bp_["bdBc"], Bp_["bdKc"], Bp_["bdV"], B_const], writes=[Bpt])
                        t2_ = kb.op("act", lambda e, ptile=ptile, tm=tm: e.activation(out=tm[:, :], in_=ptile[:, 0:384],
                                                                                    func=AF.Copy), reads=[Bpt], writes=[B_tm])
                        return t1_, t2_
                    transposes(0)
                    yield None
                    for c in range(NCH):
                        if c + 1 < NCH:
                            transposes(c + 1)
                            yield None
                        tm, B_tm = tms[c % 2], B_tms[c % 2]
                        At_c = bdc(bdA, c)
                        Rt_c = Rt[:, c * 64:(c + 1) * 64]
                        m1, B_m1 = m1s[c], B_m1s[c]
                        LakT, Mrb, Mrk = m1[:, 128:256], m1[:, 384:448], m1[:, 448:512]
                        WT, B_WT = cur[c][2], cur[c][3][0]
                        pb, Bp = next_rot()

                        def fnx(e, pb=pb, At_c=At_c, LakT=LakT, tm=tm):
                            e.matmul(pb[:, 0:128], lhsT=At_c, rhs=T16[:, :], start=True, stop=False)
                            return e.matmul(pb[:, 0:128], lhsT=LakT, rhs=tm[:, 256:384], start=False, stop=True)
                        yield kb.op("pe", fnx, reads=[Bp_["bdA"], B_T16, B_m1, B_tm], writes=[Bp])
                        yield kb.op("act", lambda e, pb=pb: e.activation(out=xu[:, 0:128], in_=pb[:, 0:128], func=AF.Copy),
                                    reads=[Bp], writes=[B_xu])
                        pb, Bp = next_rot()
                        yield kb.op("pe", lambda e, pb=pb, WT=WT: e.matmul(pb[:, 0:128], lhsT=WT, rhs=xu[:, 0:128],
                                                                           start=True, stop=True),
                                    reads=[B_WT, B_xu], writes=[Bp])
                        yield kb.op("dve", lambda e, pb=pb: e.tensor_copy(out=xu[:, 128:256], in_=pb[:, 0:128]),
                                    reads=[Bp], writes=[B_xu])

                        def fny(e, c=c, Rt_c=Rt_c, Mrb=Mrb, Mrk=Mrk, tm=tm):
                            o = yb[:, bi * TT + c * 64:bi * TT + (c + 1) * 64]
                            e.matmul(o, lhsT=T16[:, :], rhs=Rt_c, start=True, stop=False)
                            e.matmul(o, lhsT=xu[:, 128:256], rhs=Mrb, start=False, stop=False)
                            return e.matmul(o, lhsT=tm[:, 256:384], rhs=Mrk, start=False, stop=True)
                        if full:
                            yield kb.op("pe", fny, reads=[B_T16, Bp_["Rt"], B_xu, B_m1, B_tm], writes=[Byb])
                        pb, Bp = next_rot()

                        def fns(e, pb=pb, tm=tm):
                            e.matmul(pb[:, 0:128], lhsT=tm[:, 0:128], rhs=xu[:, 128:256], start=True, stop=False)
                            return e.matmul(pb[:, 0:128], lhsT=tm[:, 128:256], rhs=tm[:, 256:384], start=False,
                                            stop=True)
                        yield kb.op("pe", fns, reads=[B_tm, B_xu], writes=[Bp])
                        yield kb.op("dve", lambda e, pb=pb, p=p, c=c: e.scalar_tensor_tensor(
                            out=st32[:, p, :], in0=st32[:, p, :], scalar=nbias[:, NCH + c:NCH + c + 1],
                            in1=pb[:, 0:128], op0=ALU.mult, op1=ALU.add),
                            reads=[Bp, Bp_["nbias"], B_st[p]], writes=[B_st[p]])
                        yield kb.op("act", lambda e, p=p: e.activation(out=T16[:, :], in_=st32[:, p, :], func=AF.Copy),
                                    reads=[B_st[p]], writes=[B_T16])
                    if not full:
                        return
                    yield kb.op("act", lambda e: e.activation(out=yT[:, :], in_=yb[:, bi * TT:(bi + 1) * TT], func=AF.Copy), reads=[Byb],
                          writes=[B_yT])
                    yield kb.op("act", lambda e: e.activation(out=t16[:, :], in_=yT[:, :], func=AF.Copy), reads=[B_yT],
                          writes=[Bp_["t16"]])
                    pb, Bp = next_rot()
                    yield kb.op("pe", lambda e, pb=pb: e.matmul(pb[:, 0:TT], lhsT=bdones_b, rhs=t16[:, :], start=True,
                                                          stop=True), reads=[Bp_["t16"], B_const], writes=[Bp])
                    yield kb.op("dve", lambda e, pb=pb: e.scalar_tensor_tensor(out=pt1[:, :], in0=pb[:, 0:TT],
                                                                         scalar=-1.0 / 64, in1=yT[:, :],
                                                                         op0=ALU.mult, op1=ALU.add),
                          reads=[Bp, B_yT], writes=[B_pt1])
                    yield kb.op("act", lambda e: e.activation(out=t16[:, :], in_=pt1[:, :], func=AF.Square),
                          reads=[B_pt1], writes=[Bp_["t16"]])
                    pb, Bp = next_rot()
                    yield kb.op("pe", lambda e, pb=pb: e.matmul(pb[:, 0:TT], lhsT=bdones_b, rhs=t16[:, :], start=True,
                                                          stop=True), reads=[Bp_["t16"], B_const], writes=[Bp])
                    yield kb.op("dve", lambda e, pb=pb: e.tensor_scalar(out=pt2[:, :], in0=pb[:, 0:TT], scalar1=1.0 / 64,
                                                                  scalar2=64e-5, op0=ALU.mult, op1=ALU.add),
                          reads=[Bp], writes=[B_pt2])
                    yield kb.op("act", lambda e: e.activation(out=pt2[:, :], in_=pt2[:, :], func=AF.Sqrt), reads=[B_pt2],
                          writes=[B_pt2])
                    yield kb.op("dve", lambda e: e.reciprocal(out=pt2[:, :], in_=pt2[:, :]), reads=[B_pt2], writes=[B_pt2])
                    yield kb.op("dve", lambda e: e.tensor_tensor(out=pt1[:, :], in0=pt1[:, :], in1=pt2[:, :], op=ALU.mult),
                          reads=[B_pt1, B_pt2], writes=[B_pt1])
                    yield kb.op("dve", lambda e, p=p: e.tensor_scalar(out=pt1[:, :], in0=pt1[:, :], scalar1=pcol("lnw", p),
                                                                scalar2=pcol("lnb", p), op0=ALU.mult, op1=ALU.add),
                          reads=[B_pt1, B_const], writes=[B_pt1])
                    yield kb.op("dve", lambda e: e.tensor_tensor(out=pt1[:, :], in0=pt1[:, :], in1=bonus[:, :], op=ALU.add),
                          reads=[B_pt1, Bp_["bonus"]], writes=[B_pt1])
                    yield kb.op("dve", lambda e, p=p: e.tensor_tensor(out=orw[:, p, :], in0=pt1[:, :], in1=gv[:, :],
                                                                op=ALU.mult),
                          reads=[B_pt1, Bp_["gv"]], writes=[B_orw])

                def dense_gen(sp_, XS):
                    xs_, Bxs_ = XS
                    for wi, base in enumerate([RW0, RW0 + 4096, RW0 + 8192]):
                        if wi == 0 and not (is_main or lastpre):
                            continue
                        wt, bw = proj_in(base + sp_ * 256, 256, ("rkv", t, sp_, wi))
                        for bi in range(2):
                            acc, Ba = dense_h(wt, bw, bi * 128)
                            blk = wi * 32 + sp_ * 2 + bi
                            shift_block(acc, Ba, blk, xs_[:, wi * 2 + bi, :], Bxs_[wi * 2 + bi])
                            yield None

                for _ in dense_gen(0, XSETS[0]):
                    pass
                for sp_ in range(16):
                    segs = (wsegs("w2", 0, 128, sp_ * 256, 256, kc0=0) + wsegs("a2", 0, 128, sp_ * 256, 256, kc0=1)
                            + wsegs("g2", 0, 480, sp_ * 256, 256, kc0=2))
                    wl, bwl = ws.get(("lora", t, sp_), segs)
                    nxt = dense_gen(sp_ + 1, XSETS[(sp_ + 1) % 2]) if sp_ < 15 else None
                    if null:
                        if nxt is not None:
                            for _ in nxt:
                                pass
                        continue
                    gens = [pair_body(sp_, bi, SETS[bi], XSETS[sp_ % 2], wl, bwl, is_main) for bi in range(2)]
                    for g_ in gens:
                        for _ in range(6):
                            next(g_)
                    rnd = 0
                    while gens or nxt is not None:
                        for g_ in list(gens):
                            try:
                                next(g_)
                            except StopIteration:
                                gens.remove(g_)
                        rnd += 1
                        if nxt is not None and (rnd % 20 == 0 or not gens):
                            try:
                                next(nxt)
                            except StopIteration:
                                nxt = None
                kb.barrier()
                ckpt("RWKV", [("orw", orw[:, :, :].rearrange("p a b -> p (a b)"), B_orw, BF16)])

            if not is_main:
                mute(True)
            gm = sbt(sm, "gm", [128, NB, TT], BF16)
            B_gm = Buf("gm")
            oat = sbt(sm, "oat", [128, NB, TT], BF16)
            B_oat = Buf("oat")
            with ExitStack() as sg_:
                sig = sbt(sg_, "sig", [128, TT], F32)
                B_sig = Buf("sig")
                brt = sbt(sg_, "brt", [128, 2, TT], F32)
                B_brt = [Buf("brt0"), Buf("brt1")]
                for u in range(16):
                    wtb, bwb = ws.get(("wrb", t, u), wsegs("wrb", 0, D, u * 256, 256))
                    if not null:
                        for bi in range(2):
                            acc1, Ba1 = next_acc()
                            dense(acc1, Ba1, [(wtb, bwb, list(range(NB)), bi * 128, lambda k, pr: orw[:, k, :], [B_orw], None)])
                            kb.op("dve", lambda e, acc1=acc1, bi=bi: e.tensor_copy(out=brt[:, bi, :], in_=acc1[:, 0:TT]),
                                  reads=[Ba1], writes=[B_brt[bi]])
                    wtg, bwg = proj_in(GATE0 + D + u * 256, 256, ("grw", t, u))
                    if null:
                        continue
                    for bi in range(2):
                        j = u * 2 + bi
                        acc2, Ba2 = next_acc()
                        dense(acc2, Ba2, [(wtg, bwg, list(range(NB)), bi * 128, lambda k, pr: hT[:, k, :], [B_hT], None)])
                        kb.op("act", lambda e, acc2=acc2: e.activation(out=sig[:, :], in_=acc2[:, 0:TT], func=AF.Sigmoid),
                              reads=[Ba2], writes=[B_sig])
                        kb.op("dve", lambda e, j=j, bi=bi: e.tensor_tensor(out=gm[:, j, :], in0=brt[:, bi, :],
                                                                           in1=sig[:, :], op=ALU.mult),
                              reads=[B_brt[bi], B_sig], writes=[B_gm])
                kb.barrier()

            ckpt("GM", [("gm", gm[:, :, :].rearrange("p a b -> p (a b)"), B_gm, BF16)])
            with ExitStack() as sa:
                cosT = sbt(sa, "cosT", [128, TT], F32)
                sinT = sbt(sa, "sinT", [128, TT], F32)
                B_rope = Buf("rope")
                qb16 = sbt(sa, "qb16", [128, TT], BF16)
                B_qb = Buf("qb16")
                r1 = sbt(sa, "r1", [128, TT], F32)
                r2 = sbt(sa, "r2", [128, TT], F32)
                B_r1, B_r2 = Buf("r1"), Buf("r2")
                qT = sbt(sa, "qT", [128, 4, TT], BF16)
                B_qT = Buf("qT")
                qsq = sbt(sa, "qsq", [128, 512], BF16)
                B_qsq = Buf("qsq")
                nm = sbt(sa, "nm", [128, 512], BF16)
                B_nm = Buf("nm")
                nk = sbt(sa, "nk", [128, 2], F32)
                B_nk = Buf("nk")
                pT = sbt(sa, "pT", [128, 2, 512], BF16)
                B_pT = Buf("pT")
                psk = sbt(sa, "psk", [128, 512], BF16)
                B_psk = Buf("psk")
                rden = sbt(sa, "rden", [128, 512], F32)
                B_rden = Buf("rden")
                ksq = sbt(sa, "ksq", [128, 128], BF16)
                B_ksq = Buf("ksq")
                if lastpre:
                    mute(False)
                kb.dma("sp", cosT[:, :], T_["ropec"][:, tok0:tok0 + TT], writes=[B_rope])
                kb.dma("sp", sinT[:, :], T_["ropes"][:, tok0:tok0 + TT], writes=[B_rope])

                def rope_block(acc, Ba, bias_col, dst, B_dst):
                    kb.op("act", lambda e: e.activation(out=qb16[:, :], in_=acc[:, 0:TT], func=AF.Identity,
                                                        bias=bias_col), reads=[Ba, B_const], writes=[B_qb])
                    pb, Bp = next_rot()
                    kb.op("pe", lambda e: e.matmul(pb[:, 0:TT], lhsT=psw_b, rhs=qb16[:, :], start=True, stop=True),
                          reads=[B_qb, B_const], writes=[Bp])
                    kb.op("dve", lambda e: e.tensor_tensor(out=r1[:, :], in0=qb16[:, :], in1=cosT[:, :], op=ALU.mult),
                          reads=[B_qb, B_rope], writes=[B_r1])
                    kb.op("dve", lambda e: e.tensor_tensor(out=r2[:, :], in0=pb[:, 0:TT], in1=sinT[:, :], op=ALU.mult),
                          reads=[Bp, B_rope], writes=[B_r2])
                    kb.op("dve", lambda e: e.tensor_tensor(out=dst, in0=r1[:, :], in1=r2[:, :], op=ALU.add),
                          reads=[B_r1, B_r2], writes=[B_dst])

                for u in range(4):
                    wt, bw = proj_in(4096 + u * 256, 256, ("attk", t, u))
                    if null:
                        continue
                    for bi in range(2):
                        g = u * 2 + bi
                        acc, Ba = dense_h(wt, bw, bi * 128)
                        rope_block(acc, Ba, pcol("bqkv", 32 + g), kT[:, g, 128:128 + TT], B_kT)
                        for qb_ in range(NQB):
                            kb.op("act", lambda e, g=g, qb_=qb_: e.activation(
                                out=ksq[:, :], in_=kT[:, g, 128 + qb_ * 128:256 + qb_ * 128], func=AF.Square),
                                reads=[B_kT], writes=[B_ksq])
                            pb, Bp = next_rot()
                            kb.op("pe", lambda e, pb=pb: e.matmul(pb[:, 0:128], lhsT=ones_b, rhs=ksq[:, :], start=True,
                                                                  stop=True), reads=[B_ksq, B_const], writes=[Bp])
                            kb.op("dve", lambda e, pb=pb, g=g, qb_=qb_: e.reduce_max(
                                out=kmx[:, 1 + qb_, g:g + 1], in_=pb[:, 0:128], axis=mybir.AxisListType.X),
                                reads=[Bp], writes=[B_kmx])
                for u in range(4):
                    wt, bw = proj_in(5120 + u * 256, 256, ("attv", t, u))
                    if null:
                        continue
                    for bi in range(2):
                        g = u * 2 + bi
                        acc, Ba = dense_h(wt, bw, bi * 128)
                        kb.op("act", lambda e, acc=acc, g=g: e.activation(out=qb16[:, :], in_=acc[:, 0:TT],
                                                                          func=AF.Identity, bias=pcol("bqkv", 40 + g)),
                              reads=[Ba, B_const], writes=[B_qb])
                        ptile, Bpt = next_pt()

                        def fnv(e, ptile=ptile):
                            inst = None
                            for qb_ in range(NQB):
                                inst = e.transpose(ptile[:, qb_ * 128:(qb_ + 1) * 128],
                                                   qb16[:, qb_ * 128:(qb_ + 1) * 128], ident_b)
                            return inst
                        kb.op("pe", fnv, reads=[B_qb, B_const], writes=[Bpt])
                        kb.op("dve", lambda e, ptile=ptile, g=g: e.tensor_copy(
                            out=vtok[:, 1:1 + NQB, g, :], in_=ptile[:, 0:NQB * 128].rearrange("p (b d) -> p b d", d=128)),
                            reads=[Bpt], writes=[B_vtok])
                if not is_main:
                    mute(True)
                for g in range(8):
                    for u2 in range(2):
                        u = g * 2 + u2
                        wt, bw = proj_in(u * 256, 256, ("attq", t, u))
                        if null:
                            continue
                        for bi in range(2):
                            hq = u * 2 + bi
                            acc, Ba = dense_h(wt, bw, bi * 128)
                            rope_block(acc, Ba, pcol("bqkv", hq), qT[:, u2 * 2 + bi, :], B_qT)
                    if null:
                        continue
                    for qb_ in range(NQB):
                        has_prev = not (t == 0 and qb_ == 0)
                        first_main = (NPRE > 0 and t == NPRE and qb_ == 0)
                        qg = qT[:, :, qb_ * 128:(qb_ + 1) * 128]
                        kb.op("act", lambda e, qg=qg: e.activation(out=qsq[:, :].rearrange("p (h q) -> p h q", h=4),
                                                                   in_=qg, func=AF.Square), reads=[B_qT],
                              writes=[B_qsq])
                        pb, Bp = next_rot()
                        kb.op("pe", lambda e, pb=pb: e.matmul(pb[:, :], lhsT=ones_b, rhs=qsq[:, :], start=True, stop=True),
                              reads=[B_qsq, B_const], writes=[Bp])
                        if has_prev:
                            kb.op("dve", lambda e, g=g, qb_=qb_: e.tensor_tensor(out=nk[:, 0:1], in0=kmx[:, qb_, g:g + 1],
                                                                                 in1=kmx[:, qb_ + 1, g:g + 1], op=ALU.max),
                                  reads=[B_kmx], writes=[B_nk])
                        else:
                            kb.op("dve", lambda e, g=g, qb_=qb_: e.tensor_copy(out=nk[:, 0:1], in_=kmx[:, qb_ + 1, g:g + 1]),
                                  reads=[B_kmx], writes=[B_nk])
                        kb.op("dve", lambda e: e.tensor_scalar_mul(out=nk[:, 1:2], in0=nk[:, 0:1], scalar1=-0.5),
                              reads=[B_nk], writes=[B_nk])
                        kb.op("dve", lambda e, pb=pb: e.tensor_scalar(out=nm[:, :], in0=pb[:, :], scalar1=-0.5,
                                                                      scalar2=nk[:, 1:2], op0=ALU.mult, op1=ALU.add),
                              reads=[Bp, B_nk], writes=[B_nm])
                        for h in range(4):
                            kb.op("act", lambda e, h=h, g=g: e.activation(
                                out=psk[:, h * 128:(h + 1) * 128], in_=nm[:, h * 128:(h + 1) * 128], func=AF.Exp,
                                scale=SCALE, bias=pcol("sink", g * 4 + h)), reads=[B_nm, B_const], writes=[B_psk])
                        blocks = ([0] if has_prev else []) + [1]
                        for kbk in blocks:
                            pb, Bp = next_rot()
                            keys = kT[:, g, qb_ * 128 + kbk * 128: qb_ * 128 + kbk * 128 + 128]
                            mk = (mprev0_b if first_main else mprev_b) if kbk == 0 else mcur_b

                            def fnsc(e, pb=pb, keys=keys, mk=mk, qg=qg):
                                e.matmul(pb[:, :].rearrange("p (h q) -> p h q", h=4), lhsT=keys, rhs=qg, start=True, stop=False)
                                e.matmul(pb[:, :], lhsT=ones_b[0:1, :], rhs=nm[0:1, :], start=False, stop=False)
                                return e.matmul(pb[:, :], lhsT=ident_b, rhs=mk, start=False, stop=True)
                            kb.op("pe", fnsc, reads=[B_kT, B_qT, B_nm, B_const], writes=[Bp])
                            kb.op("act", lambda e, pb=pb, kbk=kbk: e.activation(out=pT[:, kbk, :], in_=pb[:, :],
                                                                                func=AF.Exp, scale=SCALE),
                                  reads=[Bp], writes=[B_pT])
                        pbo, Bpo = next_rot()
                        pbd, Bpd = next_rot()

                        def fnpv(e, pbo=pbo, blocks=blocks, g=g, qb_=qb_):
                            inst = None
                            for i, kbk in enumerate(blocks):
                                inst = e.matmul(pbo[:, :], lhsT=vtok[:, qb_ + kbk, g, :], rhs=pT[:, kbk, :],
                                                start=(i == 0), stop=(i == len(blocks) - 1))
                            return inst
                        kb.op("pe", fnpv, reads=[B_vtok, B_pT], writes=[Bpo])

                        def fnden(e, pbd=pbd, blocks=blocks):
                            for i, kbk in enumerate(blocks):
                                e.matmul(pbd[:, :], lhsT=ones_b, rhs=pT[:, kbk, :], start=(i == 0), stop=False)
                            return e.matmul(pbd[:, :], lhsT=ident_b, rhs=psk[:, :], start=False, stop=True)
                        kb.op("pe", fnden, reads=[B_pT, B_psk, B_const], writes=[Bpd])
                        kb.op("dve", lambda e, pbd=pbd: e.reciprocal(out=rden[:, :], in_=pbd[:, :]), reads=[Bpd],
                              writes=[B_rden])
                        kb.op("dve", lambda e, pbo=pbo, g=g, qb_=qb_: e.tensor_tensor(
                            out=oat[:, g * 4:(g + 1) * 4, qb_ * 128:(qb_ + 1) * 128],
                            in0=pbo[:, :].rearrange("p (h q) -> p h q", h=4),
                            in1=rden[:, :].rearrange("p (h q) -> p h q", h=4), op=ALU.mult),
                            reads=[Bpo, B_rden], writes=[B_oat])
                if lastpre:
                    mute(False)
                kb.op("dve", lambda e: e.tensor_copy(out=kT[:, :, 0:128], in_=kT[:, :, TT:TT + 128]), reads=[B_kT],
                      writes=[B_kT])
                kb.op("dve", lambda e: e.tensor_copy(out=vtok[:, 0, :, :], in_=vtok[:, NQB, :, :]), reads=[B_vtok],
                      writes=[B_vtok])
                kb.op("dve", lambda e: e.tensor_copy(out=kmx[:, 0, :], in_=kmx[:, NQB, :]), reads=[B_kmx], writes=[B_kmx])
                kb.barrier()
                if not is_main:
                    mute(True)

            ckpt("ATT", [("oat", oat[:, :, :].rearrange("p a b -> p (a b)"), B_oat, BF16)])
            with ExitStack() as sg_:
                sig = sbt(sg_, "sig2", [128, TT], F32)
                B_sig = Buf("sig2")
                tmpm = sbt(sg_, "tmpm", [128, TT], F32)
                B_tmpm = Buf("tmpm")
                brt = sbt(sg_, "brt2", [128, 2, TT], F32)
                B_brt = [Buf("brt20"), Buf("brt21")]
                for u in range(16):
                    wtb, bwb = ws.get(("wab", t, u), wsegs("wab", 0, D, u * 256, 256))
                    if not null:
                        for bi in range(2):
                            acc1, Ba1 = next_acc()
                            dense(acc1, Ba1, [(wtb, bwb, list(range(NB)), bi * 128, lambda k, pr: oat[:, k, :], [B_oat], None)])
                            kb.op("dve", lambda e, acc1=acc1, bi=bi: e.tensor_copy(out=brt[:, bi, :], in_=acc1[:, 0:TT]),
                                  reads=[Ba1], writes=[B_brt[bi]])
                    wtg, bwg = proj_in(GATE0 + u * 256, 256, ("gat", t, u))
                    if null:
                        continue
                    for bi in range(2):
                        j = u * 2 + bi
                        acc2, Ba2 = next_acc()
                        dense(acc2, Ba2, [(wtg, bwg, list(range(NB)), bi * 128, lambda k, pr: hT[:, k, :], [B_hT], None)])
                        kb.op("act", lambda e, acc2=acc2: e.activation(out=sig[:, :], in_=acc2[:, 0:TT], func=AF.Sigmoid),
                              reads=[Ba2], writes=[B_sig])
                        kb.op("dve", lambda e, bi=bi: e.tensor_tensor(out=tmpm[:, :], in0=brt[:, bi, :], in1=sig[:, :],
                                                                      op=ALU.mult), reads=[B_brt[bi], B_sig], writes=[B_tmpm])
                        kb.op("dve", lambda e, j=j: e.tensor_tensor(out=gm[:, j, :], in0=gm[:, j, :], in1=tmpm[:, :],
                                                                    op=ALU.add), reads=[B_tmpm, B_gm], writes=[B_gm])
                kb.barrier()

            ckpt("GM2", [("gm2", gm[:, :, :].rearrange("p a b -> p (a b)"), B_gm, BF16)])
            h2T = hT
            B_h2T = B_hT
            with ExitStack() as so:
                moT = sbt(so, "moT", [128, NB, TT], F32)
                B_mo = Buf("moT")
                sq16 = sbt(so, "sq16", [128, TT], BF16)
                B_sq = Buf("sq16")
                rb = sbt(so, "rb", [128, TT], F32)
                B_rb = Buf("rb")
                xt = sbt(so, "xt2", [128, D], F32)
                B_xt = Buf("xt2")
                stg = xt
                B_stg = B_xt
                ssb, B_ssb = pbank[YB], B_pb[YB]
                for u in range(16):
                    wt, bw = ws.get(("wout", t, u), wsegs("wout", 0, D, u * 256, 256))
                    if null:
                        continue
                    for bi in range(2):
                        j = u * 2 + bi
                        acc, Ba = next_acc()
                        dense(acc, Ba, [(wt, bw, list(range(NB)), bi * 128, lambda k, pr: gm[:, k, :], [B_gm], None)])
                        kb.op("dve", lambda e, acc=acc, j=j: e.tensor_copy(out=moT[:, j, :], in_=acc[:, 0:TT]),
                              reads=[Ba], writes=[B_mo])
                        kb.op("act", lambda e, acc=acc: e.activation(out=sq16[:, :], in_=acc[:, 0:TT], func=AF.Square),
                              reads=[Ba], writes=[B_sq])
                        kb.op("pe", lambda e, j=j: e.matmul(ssb[:, 0:TT], lhsT=ones_b, rhs=sq16[:, :], start=(j == 0),
                                                            stop=(j == NB - 1)), reads=[B_sq, B_const], writes=[B_ssb])
                ckpt("O1", [("moT", moT[:, :, :].rearrange("p a b -> p (a b)"), B_mo, F32)])
                if not null:
                    kb.op("dve", lambda e: e.tensor_scalar(out=rb[:, :], in0=ssb[:, 0:TT], scalar1=1.0 / D, scalar2=1e-6,
                                                           op0=ALU.mult, op1=ALU.add), reads=[B_ssb], writes=[B_rb])
                    kb.op("act", lambda e: e.activation(out=rb[:, :], in_=rb[:, :], func=AF.Sqrt), reads=[B_rb],
                          writes=[B_rb])
                    kb.op("dve", lambda e: e.reciprocal(out=rb[:, :], in_=rb[:, :]), reads=[B_rb], writes=[B_rb])
                    for tb in range(NQB):
                        r0 = tok0 + tb * 128
                        kb.dma("sp", xt[:, :], x[r0:r0 + 128, :], writes=[B_xt])
                        for k4 in range(NB // 4):
                            pb, Bp = next_rot()

                            def fn(e, k4=k4, pb=pb):
                                inst = None
                                for i in range(4):
                                    kc = k4 * 4 + i
                                    inst = e.matmul(pb[:, i * 128:(i + 1) * 128], lhsT=xt[:, kc * 128:(kc + 1) * 128],
                                                    rhs=ident_f, start=True, stop=True)
                                return inst
                            kb.op("pe", fn, reads=[B_xt, B_const], writes=[Bp])
                            for i in range(4):
                                kc = k4 * 4 + i
                                sl = slice(tb * 128, (tb + 1) * 128)
                                kb.op("dve", lambda e, kc=kc, sl=sl: e.scalar_tensor_tensor(
                                    out=moT[:, kc, sl], in0=moT[:, kc, sl], scalar=pcol("nmo", kc), in1=rb[:, sl],
                                    op0=ALU.mult, op1=ALU.mult), reads=[B_mo, B_rb, B_const], writes=[B_mo])
                                kb.op("dve", lambda e, kc=kc, sl=sl, pb=pb, i=i: e.tensor_tensor(
                                    out=moT[:, kc, sl], in0=moT[:, kc, sl], in1=pb[:, i * 128:(i + 1) * 128], op=ALU.add),
                                    reads=[B_mo, Bp], writes=[B_mo])
                    ckpt("O2", [("x1T", moT[:, :, :].rearrange("p a b -> p (a b)"), B_mo, F32)])
                    for j in range(NB):
                        kb.op("act", lambda e, j=j: e.activation(out=sq16[:, :], in_=moT[:, j, :], func=AF.Square),
                              reads=[B_mo], writes=[B_sq])
                        kb.op("pe", lambda e, j=j: e.matmul(ssb[:, 0:TT], lhsT=ones_b, rhs=sq16[:, :], start=(j == 0),
                                                            stop=(j == NB - 1)), reads=[B_sq, B_const], writes=[B_ssb])
                    kb.op("dve", lambda e: e.tensor_scalar(out=rb[:, :], in0=ssb[:, 0:TT], scalar1=1.0 / D, scalar2=1e-6,
                                                           op0=ALU.mult, op1=ALU.add), reads=[B_ssb], writes=[B_rb])
                    kb.op("act", lambda e: e.activation(out=rb[:, :], in_=rb[:, :], func=AF.Sqrt), reads=[B_rb],
                          writes=[B_rb])
                    kb.op("dve", lambda e: e.reciprocal(out=rb[:, :], in_=rb[:, :]), reads=[B_rb], writes=[B_rb])
                    for j in range(NB):
                        kb.op("dve", lambda e, j=j: e.scalar_tensor_tensor(out=h2T[:, j, :], in0=moT[:, j, :],
                                                                           scalar=pcol("nfp", j), in1=rb[:, :],
                                                                           op0=ALU.mult, op1=ALU.mult),
                              reads=[B_mo, B_rb, B_const], writes=[B_h2T])
                    ckpt("O3", [("h2T", hT[:, :, :].rearrange("p a b -> p (a b)"), B_hT, BF16)])
                    for tb in range(NQB):
                        r0 = tok0 + tb * 128
                        for k4 in range(NB // 4):
                            pb, Bp = next_rot()

                            def fn(e, k4=k4, pb=pb, tb=tb):
                                inst = None
                                for i in range(4):
                                    kc = k4 * 4 + i
                                    inst = e.matmul(pb[:, i * 128:(i + 1) * 128], lhsT=moT[:, kc, tb * 128:(tb + 1) * 128],
                                                    rhs=ident_f, start=True, stop=True)
                                return inst
                            kb.op("pe", fn, reads=[B_mo, B_const], writes=[Bp])
                            if k4 % 2:
                                kb.op("act", lambda e, k4=k4, pb=pb: e.activation(out=stg[:, k4 * 512:(k4 + 1) * 512],
                                                                                   in_=pb[:, :], func=AF.Copy),
                                      reads=[Bp], writes=[B_stg])
                            else:
                                kb.op("dve", lambda e, k4=k4, pb=pb: e.tensor_copy(out=stg[:, k4 * 512:(k4 + 1) * 512],
                                                                                    in_=pb[:, :]), reads=[Bp], writes=[B_stg])
                        kb.dma("sp", out[otok0 + tb * 128:otok0 + tb * 128 + 128, :], stg[:, :], reads=[B_stg])
                kb.barrier()

        ckpt("O", [("h2T", hT[:, :, :].rearrange("p a b -> p (a b)"), B_hT, BF16)])
        with ExitStack() as sf:
            aT = sbt(sf, "aT", [128, NFB, TT], BF16)
            B_aT = Buf("aT")
            with ExitStack() as sf1:
                sgl = sbt(sf1, "sgl", [128, 2, TT], F32)
                B_sgl = [Buf("sgl0"), Buf("sgl1")]
                for u in range(FFN // 256):
                    wtg, bwg = ws.get(("ffg", t, u), wsegs("wg", 0, D, u * 256, 256))
                    if not null:
                        for bi in range(2):
                            acc1, Ba1 = next_acc()
                            dense(acc1, Ba1, [(wtg, bwg, list(range(NB)), bi * 128, lambda k, pr: h2T[:, k, :], [B_h2T], None)])
                            kb.op("act", lambda e, acc1=acc1, bi=bi: e.activation(out=sgl[:, bi, :], in_=acc1[:, 0:TT],
                                                                                  func=AF.Silu),
                                  reads=[Ba1], writes=[B_sgl[bi]])
                    wtu, bwu = ws.get(("ffu", t, u), wsegs("wu", 0, D, u * 256, 256))
                    if null:
                        continue
                    for bi in range(2):
                        j = u * 2 + bi
                        acc2, Ba2 = next_acc()
                        dense(acc2, Ba2, [(wtu, bwu, list(range(NB)), bi * 128, lambda k, pr: h2T[:, k, :], [B_h2T], None)])
                        kb.op("dve", lambda e, acc2=acc2, j=j, bi=bi: e.tensor_tensor(out=aT[:, j, :], in0=acc2[:, 0:TT],
                                                                                      in1=sgl[:, bi, :], op=ALU.mult),
                              reads=[Ba2, B_sgl[bi]], writes=[B_aT])
                kb.barrier()
            with ExitStack() as sf2:
                fT = sbt(sf2, "fT", [128, NB, TT], F32)
                B_fT = Buf("fT")
                sq16 = sbt(sf2, "sq16b", [128, TT], BF16)
                B_sq = Buf("sq16b")
                rb = sbt(sf2, "rb2", [128, TT], F32)
                B_rb = Buf("rb2")
                rtok = sbt(sf2, "rtok", [128, NQB], F32)
                B_rtok = Buf("rtok")
                x1c = sbt(sf2, "x1c", [128, 1024], F32)
                B_x1c = Buf("x1c")
                oc = sbt(sf2, "oc", [128, 1024], F32)
                B_oc = Buf("oc")
                ssb, B_ssb = pbank[YB], B_pb[YB]
                kparts = [(0, 32), (32, 32), (64, NFB - 64)]
                for u in range(16):
                    accs = [next_acc(), next_acc()]
                    for pi, (k0, nk_) in enumerate(kparts):
                        wt, bw = ws.get(("ffd", t, u, pi), wsegs("wd", k0 * 128, nk_ * 128, u * 256, 256))
                        if null:
                            continue
                        for bi in range(2):
                            acc, Ba = accs[bi]

                            def fnd(e, acc=acc, wt=wt, bi=bi, k0=k0, nk_=nk_, pi=pi):
                                inst = None
                                for k in range(nk_):
                                    inst = e.matmul(acc[:, 0:TT], lhsT=wt[:, k, bi * 128:(bi + 1) * 128], rhs=aT[:, k0 + k, :],
                                                    start=(pi == 0 and k == 0), stop=(pi == 2 and k == nk_ - 1))
                                return inst
                            kb.op("pe", fnd, reads=[bw, B_aT], writes=[Ba])
                    if null:
                        continue
                    for bi in range(2):
                        j = u * 2 + bi
                        acc, Ba = accs[bi]
                        kb.op("dve", lambda e, acc=acc, j=j: e.tensor_scalar_mul(out=fT[:, j, :], in0=acc[:, 0:TT],
                                                                                 scalar1=pcol("nfo", j)),
                              reads=[Ba, B_const], writes=[B_fT])
                        kb.op("act", lambda e, acc=acc: e.activation(out=sq16[:, :], in_=acc[:, 0:TT], func=AF.Square),
                              reads=[Ba], writes=[B_sq])
                        kb.op("pe", lambda e, j=j: e.matmul(ssb[:, 0:TT], lhsT=ones_b, rhs=sq16[:, :], start=(j == 0),
                                                            stop=(j == NB - 1)), reads=[B_sq, B_const], writes=[B_ssb])
                if not null:
                    kb.op("dve", lambda e: e.tensor_scalar(out=rb[:, :], in0=ssb[:, 0:TT], scalar1=1.0 / D, scalar2=1e-6,
                                                           op0=ALU.mult, op1=ALU.add), reads=[B_ssb], writes=[B_rb])
                    kb.op("act", lambda e: e.activation(out=rb[:, :], in_=rb[:, :], func=AF.Sqrt), reads=[B_rb],
                          writes=[B_rb])
                    kb.op("dve", lambda e: e.reciprocal(out=rb[:, :], in_=rb[:, :]), reads=[B_rb], writes=[B_rb])
                    for j in range(NB):
                        kb.op("dve", lambda e, j=j: e.tensor_tensor(out=fT[:, j, :], in0=fT[:, j, :], in1=rb[:, :],
                                                                    op=ALU.mult), reads=[B_fT, B_rb], writes=[B_fT])
                    for tb in range(NQB):
                        r0 = tok0 + tb * 128
                        for q4 in range(4):
                            kb.dma("sp", x1c[:, :], out[otok0 + tb * 128:otok0 + tb * 128 + 128, q4 * 1024:(q4 + 1) * 1024],
                                   writes=[B_x1c])
                            for h2 in range(2):
                                pb, Bp = next_rot()

                                def fn(e, pb=pb, q4=q4, h2=h2, tb=tb):
                                    inst = None
                                    for i in range(4):
                                        kc = q4 * 8 + h2 * 4 + i
                                        inst = e.matmul(pb[:, i * 128:(i + 1) * 128],
                                                        lhsT=fT[:, kc, tb * 128:(tb + 1) * 128], rhs=ident_f, start=True,
                                                        stop=True)
                                    return inst
                                kb.op("pe", fn, reads=[B_fT, B_const], writes=[Bp])
                                kb.op("dve", lambda e, pb=pb, h2=h2: e.tensor_tensor(
                                    out=oc[:, h2 * 512:(h2 + 1) * 512], in0=pb[:, :], in1=x1c[:, h2 * 512:(h2 + 1) * 512],
                                    op=ALU.add), reads=[Bp, B_x1c], writes=[B_oc])
                            kb.dma("sp", out[otok0 + tb * 128:otok0 + tb * 128 + 128, q4 * 1024:(q4 + 1) * 1024], oc[:, :],
                                   reads=[B_oc])
                kb.barrier()
        stile.close()
    mute(False)
    kb.finish()


def _dram(nc, NPRE, NT):
    def din(name, shape, dt=F32):
        return nc.dram_tensor(name, shape, dt, kind="ExternalInput").ap()
    NA = NPRE + NT
    T_ = {
        "x": din("x", [NA * TT, D]),
        "w_in": din("w_in", [D, IN_DIM]),
        "w2": din("w2", [128, D]), "a2": din("a2", [128, D]), "g2": din("g2", [480, D]),
        "wab": din("wab", [D, D]), "wrb": din("wrb", [D, D]), "wout": din("wout", [D, D]),
        "wg": din("wg", [D, FFN]), "wu": din("wu", [D, FFN]), "wd": din("wd", [FFN, D]),
        "pvec": din("pvec", [128, NPV]), "cbf": din("cbf", [128, NCBF], BF16), "cf32": din("cf32", [128, NCF]),
        "ropec": din("ropec", [128, NA * TT]), "ropes": din("ropes", [128, NA * TT]),
    }
    T_["out"] = nc.dram_tensor("out", [NT * TT, D], F32, kind="ExternalOutput").ap()
    if NPRE > 0:
        for i_ in range(3):
            T_["wsc%d" % i_] = nc.dram_tensor("wsc%d" % i_, [NUSCR // 3, 128, 32 * 256], BF16, kind="Internal").ap()
    return T_


def build(NPRE, NT):
    ws = WStream()
    nc0 = bass.Bass("TRN2", target_bir_lowering=False)
    with ExitStack() as es0:
        emit_program(nc0, KB(nc0, es0, null=True), ws, es0, _dram(nc0, NPRE, NT), NPRE, NT)
    nc = bass.Bass("TRN2", target_bir_lowering=False)
    with ExitStack() as es:
        kb = KB(nc, es)
        emit_program(nc, kb, ws, es, _dram(nc, NPRE, NT), NPRE, NT)
    return nc


def _colmajor(v, nblk):
    v = np.asarray(v, np.float32).reshape(-1)
    o = np.zeros(nblk * 128, np.float32)
    o[:v.size] = v
    return o.reshape(nblk, 128).T


def _host_consts(pos, first_has_prev):
    cbf = np.zeros((128, NCBF), np.float32)
    cbf[:, CB_ID:CB_ID + 128] = np.eye(128)
    cbf[:, CB_ONES:CB_ONES + 128] = 1.0
    p = np.arange(128)
    cbf[:, CB_BDONES:CB_BDONES + 128] = (p[:, None] // 64 == p[None, :] // 64)
    cbf[:, CB_PSW:CB_PSW + 128] = (p[:, None] == (p[None, :] + 64) % 128)
    mprev = np.where(p[:, None] > p[None, :], 0.0, MASKNEG)
    mcur = np.where(p[:, None] <= p[None, :], 0.0, MASKNEG)
    cbf[:, CB_MPREV:CB_MPREV + 512] = np.tile(mprev, (1, 4))
    cbf[:, CB_MCUR:CB_MCUR + 512] = np.tile(mcur, (1, 4))
    cbf[:, CB_MPREV0:CB_MPREV0 + 512] = np.tile(mprev, (1, 4)) if first_has_prev else MASKNEG
    same = (p[:, None] // 64 == p[None, :] // 64)
    lt = same & (p[:, None] % 64 < p[None, :] % 64)
    gt = same & (p[:, None] % 64 > p[None, :] % 64)
    incl = (p[:, None] % 64 <= np.arange(64)[None, :])
    cbf[:, CB_MCAT:CB_MCAT + 512] = np.concatenate([lt, lt, gt, incl, incl], axis=1)
    bdm = np.zeros((128, NCH, 2, 64), np.float32)
    bdm[:64, :, 0, :] = 1.0
    bdm[64:, :, 1, :] = 1.0
    cbf[:, CB_BDM:CB_BDM + NCH * 128] = bdm.reshape(128, -1)
    cf = np.zeros((128, NCF), np.float32)
    cf[:, CF_ID:CF_ID + 128] = np.eye(128)
    seg = np.ones(TT, np.float32)
    seg[::64] = 0.0
    cf[:, CF_SEG:CF_SEG + TT] = seg[None, :]
    pos = np.asarray(pos, np.float32)
    inv = (np.float32(10000.0) ** (-np.arange(0, 128, 2, dtype=np.float32) / np.float32(128))).astype(np.float32)
    ang = (pos[:, None] * inv[None, :]).astype(np.float32)
    c = np.cos(ang).astype(np.float32).T
    s_ = np.sin(ang).astype(np.float32).T
    ropec = np.concatenate([c, c], axis=0)
    ropes = np.concatenate([-s_, s_], axis=0)
    return cbf.astype(ml_dtypes.bfloat16), cf, np.ascontiguousarray(ropec), np.ascontiguousarray(ropes)


def _pvec(inp):
    pv = np.zeros((128, NPV), np.float32)

    def put(name, v, n):
        pv[:, PV[name]:PV[name] + n] = _colmajor(v, n)
    put("nmp", inp["norm_mix_pre"][0], 32)
    put("nmo", inp["norm_mix_post"][0], 32)
    put("nfp", inp["norm_ffn_pre"][0], 32)
    put("nfo", inp["norm_ffn_post"][0], 32)
    put("bqkv", inp["b_qkv"][0], 48)
    put("mu", inp["mu_shift"][0], 102)
    put("w0", inp["w0"][0], 32)
    put("a0", inp["a0"][0], 32)
    put("kk", inp["k_k"][0], 32)
    put("ka", inp["k_a"][0], 32)
    put("lnw", inp["ln_x_w"][0], 32)
    put("lnb", inp["ln_x_b"][0], 32)
    put("rk", inp["r_k"][0], 32)
    pv[:, PV["sink"]:PV["sink"] + 32] = np.asarray(inp["att_sinks"][0], np.float32)[None, :]
    return pv


def run(inp, seqs, nsplit, trace=False):
    S = seqs[0].shape[0]
    NT = S // nsplit // TT
    NPRE = (nsplit - 1) * NT
    nc = build(NPRE, NT)
    pv = _pvec(inp)
    f = lambda k: np.ascontiguousarray(np.asarray(inp[k], np.float32)[0])
    common = {
        "w_in": f("w_in"), "w2": f("w2"), "a2": f("a2"), "g2": f("g2"),
        "wab": f("w_att_branch"), "wrb": f("w_rwkv_branch"), "wout": f("w_out"),
        "wg": f("w_ffn_gate"), "wu": f("w_ffn_up"), "wd": f("w_ffn_down"), "pvec": pv,
    }
    in_maps = []
    for xseq in seqs:
        xseq = np.asarray(xseq, np.float32)
        for h in range(nsplit):
            lo = h * NT * TT
            npad = (NPRE * TT) - lo
            xc = np.zeros(((NPRE + NT) * TT, D), np.float32)
            xc[npad:] = xseq[0:lo + NT * TT]
            pos = np.concatenate([np.zeros(npad, np.float32), np.arange(lo + NT * TT, dtype=np.float32)])
            cbf, cf, ropec, ropes = _host_consts(pos, first_has_prev=(h > 0))
            m = dict(common)
            m.update({"x": xc, "cbf": cbf, "cf32": cf, "ropec": ropec, "ropes": ropes})
            in_maps.append(m)
    res = run_bass_kernel_spmd(nc, in_maps, core_ids=list(range(len(in_maps))), trace=trace)
    DBG["res"] = res.results
    outs = []
    for b in range(len(seqs)):
        outs.append(np.concatenate([res.results[b * nsplit + h]["out"] for h in range(nsplit)], axis=0))
    return outs, res


def kernel(**inputs):
    x = np.asarray(inputs["x"], np.float32)
    B = x.shape[0]
    outs, _ = run(inputs, [x[b] for b in range(B)], 2)
    return np.stack(outs, axis=0).astype(np.float32)
```

```python
import numpy as np
import ml_dtypes
from contextlib import ExitStack
import concourse.bass as bass
import concourse.mybir as mybir
from concourse.bass_utils import run_bass_kernel_spmd

F32 = mybir.dt.float32
BF16 = mybir.dt.bfloat16
AF = mybir.ActivationFunctionType
ALU = mybir.AluOpType

D = 4096
NB = 32
TT = 256
NCH = TT // 64
NQB = TT // 128
ATT_QKV = 6144
RW0 = 6144
RWS = RW0 + 12288
GATE0 = RW0 + 13024
IN_DIM = 27360
FFN = 11008
NFB = FFN // 128
KAPPA = float(np.exp(-0.5))
SCALE = float(128 ** -0.5)
NSLOT = 3
NUSCR = 321
MASKNEG = -30000.0

PV = {}
_o = 0
for _n, _c in [("nmp", 32), ("nmo", 32), ("nfp", 32), ("nfo", 32), ("bqkv", 48), ("mu", 102), ("w0", 32),
               ("a0", 32), ("kk", 32), ("ka", 32), ("lnw", 32), ("lnb", 32), ("rk", 32), ("sink", 32),
               ("omka", 32)]:
    PV[_n] = _o
    _o += _c
NPV = _o
CB_ID, CB_ONES, CB_BDONES, CB_PSW = 0, 128, 256, 384
CB_MPREV, CB_MCUR, CB_MCAT, CB_BDM = 512, 1024, 1536, 2048
CB_MPREV0 = 2048 + NCH * 128
NCBF = CB_MPREV0 + 512
CF_ID, CF_SEG = 0, 128
NCF = 128 + TT


class StopEmit(Exception):
    pass


DBG = {"stop": None, "dumps": []}


class Buf:
    __slots__ = ("w", "r", "name", "excl")

    def __init__(self, name="", excl=False):
        self.w = None
        self.r = {}
        self.name = name
        self.excl = excl


class Eng:
    def __init__(self, name, e, sem):
        self.name = name
        self.e = e
        self.sem = sem
        self.count = 0
        self.waited = {}


class KB:
    def __init__(self, nc, es, null=False):
        self.nc = nc
        self.null = null
        self.engs = {}
        if not null:
            for name, e in [("pe", nc.tensor), ("act", nc.scalar), ("dve", nc.vector), ("pool", nc.gpsimd),
                            ("sp", nc.sync)]:
                self.engs[name] = Eng(name, e, es.enter_context(nc.semaphore("s_" + name)))
            self.dsem = {}
            for q, n in [("sp", 8), ("pool", NSLOT + 3)]:
                self.dsem[q] = [[es.enter_context(nc.semaphore("d_%s%d" % (q, i))), 0] for i in range(n)]
            self.drr = {"sp": 0, "pool": 0}
            self.sp_out = {}

    def _wait(self, eng, t):
        if t is None:
            return
        key, sem, val = t
        if key == eng.name and key == "pe":
            return
        if eng.waited.get(key, 0) >= val:
            return
        eng.e.wait_ge(sem, val)
        eng.waited[key] = val

    def _deps(self, eng, reads, writes):
        for b in reads:
            self._wait(eng, b.w)
            if b.excl:
                for t in b.r.values():
                    if t[0] != eng.name:
                        self._wait(eng, t)
        for b in writes:
            self._wait(eng, b.w)
            for t in b.r.values():
                self._wait(eng, t)

    def _mark(self, t, reads, writes):
        for b in writes:
            b.w = t
            b.r = {}
        for b in reads:
            b.r[t[0]] = t

    def op(self, en, fn, reads=(), writes=()):
        if self.null:
            return None
        eng = self.engs[en]
        self._deps(eng, reads, writes)
        inst = fn(eng.e)
        eng.count += 1
        inst.then_inc(eng.sem, 1)
        t = (en, eng.sem, eng.count)
        self._mark(t, reads, writes)
        return t

    def dma(self, q, out, in_, reads=(), writes=()):
        if self.null:
            return None
        eng = self.engs[q]
        self._deps(eng, reads, writes)
        slots = self.dsem[q]
        i = self.drr[q]
        self.drr[q] = (i + 1) % len(slots)
        sem, cnt = slots[i]
        key = "d_%s%d" % (q, i)
        if cnt > 0:
            self._wait(eng, (key, sem, cnt))
        eng.e.dma_start(out=out, in_=in_).then_inc(sem, 16)
        slots[i][1] = cnt + 16
        t = (key, sem, cnt + 16)
        self._mark(t, reads, writes)
        if q == "sp":
            self.sp_out[key] = t
        return t

    def barrier(self, names=("pe", "act", "dve", "sp")):
        if self.null:
            return
        for n in names:
            eng = self.engs[n]
            for m in names:
                if m == n:
                    continue
                o = self.engs[m]
                if o.count > 0:
                    self._wait(eng, (m, o.sem, o.count))
            for t in self.sp_out.values():
                self._wait(eng, t)

    def finish(self):
        if self.null:
            return
        sp = self.engs["sp"]
        for n in ("pe", "act", "dve"):
            o = self.engs[n]
            if o.count > 0:
                self._wait(sp, (n, o.sem, o.count))
        for t in self.sp_out.values():
            self._wait(sp, t)
        for q in ("pool",):
            for i, (sem, cnt) in enumerate(self.dsem[q]):
                if cnt > 0:
                    self._wait(sp, ("d_%s%d" % (q, i), sem, cnt))
        if getattr(self, "cvsem", None) is not None:
            sem, ws_ = self.cvsem
            if ws_.cv_issued > 0:
                sp.e.wait_ge(sem, 16 * ws_.cv_issued)


class WStream:
    def __init__(self):
        self.plan = []
        self.record = True
        self.used = 0
        self.issued = 0

    def start_emit(self, kb, slots, T_, npre=0, cvsem=None):
        self.record = False
        self.muted = False
        self.kb = kb
        self.T_ = T_
        self.npre = npre
        self.cvsem = cvsem
        self.cv_issued = 0
        self.cv_waited = False
        self.cv_plan = []
        self.sidx = {}
        self.slots = slots
        self.used = 0
        self.issued = 0
        if npre > 0:
            for key, segs in self.plan:
                if key[1] != npre or key[0] == "lora":
                    continue
                idx = len(self.sidx)
                self.sidx[(key[0],) + tuple(key[2:])] = idx
                for sg_ in segs:
                    nkc = sg_[7]
                    h = (nkc + 1) // 2
                    self.cv_plan.append((idx, sg_, 0, h))
                    if nkc > h:
                        self.cv_plan.append((idx, sg_, h, nkc))

    def _scr(self, idx):
        per = NUSCR // 3
        assert idx < NUSCR
        return self.T_["wsc%d" % (idx // per)][idx % per]

    def _convert(self, n):
        eng = self.kb.engs["pool"]
        while n > 0 and self.cv_issued < len(self.cv_plan):
            j = self.cv_issued
            idx, (W, r0, nr, c0, ncols, prow, kc0, nkc, cdst), kA, kB = self.cv_plan[j]
            if j >= 1:
                eng.e.wait_ge(self.cvsem, 16 * j)
            src = self.T_[W][r0 + kA * 128:r0 + kB * 128, c0:c0 + ncols].rearrange("(k p) c -> p k c", p=128)
            dst = self._scr(idx).rearrange("p (k c) -> p k c", c=256)[:, kc0 + kA:kc0 + kB, cdst:cdst + ncols]
            eng.e.dma_start(out=dst, in_=src).then_inc(self.cvsem, 16)
            self.cv_issued += 1
            n -= 1

    def get(self, key, segs):
        if getattr(self, "muted", False):
            return (self.slots[0] if not self.record else (None, None))
        if self.record:
            self.plan.append((key, segs))
            return None, None
        u = self.used
        assert self.plan[u][0] == key, (key, self.plan[u][0])
        while self.issued < min(len(self.plan), u + NSLOT):
            self._issue(self.issued)
            self.issued += 1
        self.used += 1
        return self.slots[u % NSLOT]

    def _issue(self, i):
        key, segs = self.plan[i]
        tile, buf = self.slots[i % NSLOT]
        use_bf = self.npre > 0 and key[1] >= self.npre
        if use_bf and not self.cv_waited:
            self._convert(len(self.cv_plan))
            self.kb.engs["pool"].e.wait_ge(self.cvsem, 16 * len(self.cv_plan))
            self.cv_waited = True
        skey = (key[0],) + tuple(key[2:])
        if use_bf and skey in self.sidx:
            kcm = max(sg_[6] + sg_[7] for sg_ in segs)
            src = self._scr(self.sidx[skey])[:, 0:kcm * 256].rearrange("p (k c) -> p k c", c=256)
            self.kb.dma("pool", tile[:, 0:kcm, :], src, writes=[buf])
        else:
            for (W, r0, nr, c0, ncols, prow, kc0, nkc, cdst) in segs:
                src = self.T_[W][r0:r0 + nr, c0:c0 + ncols].rearrange("(k p) c -> p k c", p=prow)
                self.kb.dma("pool", tile[0:prow, kc0:kc0 + nkc, cdst:cdst + ncols], src, writes=[buf])
        if self.npre > 0 and not use_bf:
            self._convert(2)


def wsegs(W, r0, nrows, c0, ncols, kc0=0, cdst=0):
    segs = []
    nfull = nrows // 128
    if nfull:
        segs.append((W, r0, nfull * 128, c0, ncols, 128, kc0, nfull, cdst))
    rem = nrows - nfull * 128
    if rem:
        segs.append((W, r0 + nfull * 128, rem, c0, ncols, rem, kc0 + nfull, 1, cdst))
    return segs


def emit_program(nc, kb, ws, es, T_, NPRE, NT):
    null = kb.null

    def mute(flag):
        kb.null = True if null else flag
        ws.muted = flag

    uniq = [0]

    def sbt(st, name, shape, dt):
        uniq[0] += 1
        return st.enter_context(nc.sbuf_tensor("%s_%d" % (name, uniq[0]), shape, dt))

    x, out = T_["x"], T_["out"]
    w_in = T_["w_in"]

    pv = sbt(es, "pv", [128, NPV], F32)
    cb = sbt(es, "cb", [128, NCBF], BF16)
    cf = sbt(es, "cf", [128, NCF], F32)
    B_const = Buf("const")
    wslots = []
    for i in range(NSLOT):
        wslots.append((sbt(es, "wslot%d" % i, [128, 32, 256], BF16), Buf("w%d" % i)))
    if not null:
        cvsem = es.enter_context(nc.semaphore("cvsem"))
        ws.start_emit(kb, wslots, T_, NPRE, cvsem)
        kb.cvsem = (cvsem, ws)
    st32 = sbt(es, "st32", [128, 32, 128], F32)
    B_st = [Buf("st%d" % p) for p in range(32)]
    carry = sbt(es, "carry", [128, 104], F32)
    B_carry = Buf("carry")
    kT = sbt(es, "kT", [128, 8, 128 + TT], BF16)
    B_kT = Buf("kT")
    vtok = sbt(es, "vtok", [128, 1 + NQB, 8, 128], BF16)
    B_vtok = Buf("vtok")
    kmx = sbt(es, "kmx", [128, 1 + NQB, 8], F32)
    B_kmx = Buf("kmx")
    pbank = [es.enter_context(nc.psum_tensor("pb%d" % i, [128, 512], F32)) for i in range(7)]
    ptb = es.enter_context(nc.psum_tensor("ptb", [128, 1024], BF16))
    B_pb = [Buf("pb%d" % i, excl=True) for i in range(7)]
    _bpt = Buf("pt", excl=True)
    B_pt = [_bpt, _bpt]
    ACC = [0, 1]
    YB = 2
    ROT = [3, 4, 5, 6]
    state = {"acc": 0, "rot": 0, "pt": 0, "ew": 0}

    def next_acc():
        i = ACC[state["acc"] % 2]
        state["acc"] += 1
        return pbank[i], B_pb[i]

    def next_rot():
        i = ROT[state["rot"] % len(ROT)]
        state["rot"] += 1
        return pbank[i], B_pb[i]

    def next_pt():
        i = state["pt"] % 2
        state["pt"] += 1
        return ptb[:, i * 512:(i + 1) * 512], B_pt[i]

    def ew():
        state["ew"] += 1
        return "dve" if state["ew"] % 2 else "act"

    ident_f = cf[:, CF_ID:CF_ID + 128]
    segmask = cf[:, CF_SEG:CF_SEG + TT]
    ident_b = cb[:, CB_ID:CB_ID + 128]
    ones_b = cb[:, CB_ONES:CB_ONES + 128]
    bdones_b = cb[:, CB_BDONES:CB_BDONES + 128]
    psw_b = cb[:, CB_PSW:CB_PSW + 128]
    mprev_b = cb[:, CB_MPREV:CB_MPREV + 512]
    mcur_b = cb[:, CB_MCUR:CB_MCUR + 512]
    mcat_b = cb[:, CB_MCAT:CB_MCAT + 512]
    bdm_b = cb[:, CB_BDM:CB_BDM + NCH * 128].rearrange("p (c h t) -> p c h t", c=NCH, h=2)

    def pcol(name, j):
        o = PV[name] + j
        return pv[:, o:o + 1]

    def ckpt(label, dumps=()):
        if label in DBG.get("dump_at", ()) and not kb.null:
            for (name, ap, buf, dt) in dumps:
                dd = nc.dram_tensor("dbg_" + name, list(ap.shape), dt, kind="ExternalOutput").ap()
                kb.dma("sp", dd, ap, reads=[buf])
            return
        if DBG["stop"] != label:
            return
        if not null:
            for (name, ap, buf, dt) in dumps:
                shp = list(ap.shape)
                dd = nc.dram_tensor("dbg_" + name, shp, dt, kind="ExternalOutput").ap()
                kb.dma("sp", dd, ap, reads=[buf])
                DBG["dumps"].append("dbg_" + name)
        kb.null = True
        ws.muted = True

    kb.dma("sp", pv[:, :], T_["pvec"][:, :], writes=[B_const])
    kb.dma("sp", cb[:, :], T_["cbf"][:, :], writes=[B_const])
    kb.dma("sp", cf[:, :], T_["cf32"][:, :], writes=[B_const])
    kb.op("dve", lambda e: e.tensor_scalar(out=pv[:, PV["omka"]:PV["omka"] + 32], in0=pv[:, PV["ka"]:PV["ka"] + 32],
                                           scalar1=-1.0, scalar2=1.0, op0=ALU.mult, op1=ALU.add),
          reads=[B_const], writes=[B_const])
    kb.op("dve", lambda e: e.memset(st32[:, :, :], 0.0), writes=B_st)
    kb.op("dve", lambda e: e.memset(carry[:, :], 0.0), writes=[B_carry])
    kb.op("dve", lambda e: e.memset(kT[:, :, :], 0.0), writes=[B_kT])
    kb.op("dve", lambda e: e.memset(vtok[:, :, :, :], 0.0), writes=[B_vtok])
    kb.op("dve", lambda e: e.memset(kmx[:, :, :], 0.0), writes=[B_kmx])
    kb.barrier()

    def dense(acc, B_acc, parts):
        n = sum(len(p[2]) for p in parts)

        def fn(e):
            i = 0
            inst = None
            for (wt, bw, kcs, c0, rhs_fn, rb, prows) in parts:
                for j, kc in enumerate(kcs):
                    pr = prows[j] if prows is not None else 128
                    inst = e.matmul(acc[:, 0:TT], lhsT=wt[0:pr, kc, c0:c0 + 128], rhs=rhs_fn(j, pr),
                                    start=(i == 0), stop=(i == n - 1))
                    i += 1
            return inst
        reads = []
        for p in parts:
            reads.append(p[1])
            reads.extend(p[5])
        kb.op("pe", fn, reads=reads, writes=[B_acc])

    mprev0_b = cb[:, CB_MPREV0:CB_MPREV0 + 512]
    for t in range(NPRE + NT):
        tok0 = t * TT
        otok0 = max(0, t - NPRE) * TT
        is_main = t >= NPRE
        lastpre = (t == NPRE - 1)
        mute(False)
        stile = ExitStack()
        DBG["stile"] = stile
        hT = sbt(stile, "hT", [128, NB, TT], BF16)
        B_hT = Buf("hT")
        with ExitStack() as sm:
            orw = sbt(sm, "orw", [128, NB, TT], BF16)
            B_orw = Buf("orw")
            h_rhs = (lambda j, pr: None)

            with ExitStack() as s0:
                xt = sbt(s0, "xt", [128, D], F32)
                B_xt = Buf("xt")
                ssq = sbt(s0, "ssq", [128, 2], F32)
                B_ssq = Buf("ssq")
                dg = sbt(s0, "dg", [128, 128], F32)
                B_dg = Buf("dg")
                for tb in range(NQB):
                    r0 = tok0 + tb * 128
                    kb.dma("sp", xt[:, :], x[r0:r0 + 128, :], writes=[B_xt])
                    kb.op("act", lambda e: e.activation(out=orw[:, 0:16, :].rearrange("p a b -> p (a b)"),
                                                        in_=xt[:, :], func=AF.Square, accum_out=ssq[:, 0:1]),
                          reads=[B_xt], writes=[B_orw, B_ssq])
                    kb.op("dve", lambda e: e.tensor_scalar(out=ssq[:, 1:2], in0=ssq[:, 0:1], scalar1=1.0 / D,
                                                           scalar2=1e-6, op0=ALU.mult, op1=ALU.add),
                          reads=[B_ssq], writes=[B_ssq])
                    kb.op("act", lambda e: e.activation(out=ssq[:, 1:2], in_=ssq[:, 1:2], func=AF.Sqrt),
                          reads=[B_ssq], writes=[B_ssq])
                    kb.op("dve", lambda e: e.reciprocal(out=ssq[:, 1:2], in_=ssq[:, 1:2]), reads=[B_ssq],
                          writes=[B_ssq])
                    kb.op("dve", lambda e: e.tensor_scalar_mul(out=dg[:, :], in0=ident_f, scalar1=ssq[:, 1:2]),
                          reads=[B_ssq, B_const], writes=[B_dg])
                    for k4 in range(NB // 4):
                        pb, Bp = next_rot()

                        def fn(e, k4=k4, pb=pb):
                            inst = None
                            for i in range(4):
                                kc = k4 * 4 + i
                                inst = e.matmul(pb[:, i * 128:(i + 1) * 128], lhsT=xt[:, kc * 128:(kc + 1) * 128],
                                                rhs=dg[:, :], start=True, stop=True)
                            return inst
                        kb.op("pe", fn, reads=[B_xt, B_dg], writes=[Bp])
                        for i in range(4):
                            kc = k4 * 4 + i
                            en = ew()
                            if en == "act":
                                kb.op("act", lambda e, kc=kc, i=i, pb=pb, tb=tb: e.activation(
                                    out=hT[:, kc, tb * 128:(tb + 1) * 128], in_=pb[:, i * 128:(i + 1) * 128],
                                    func=AF.Identity, scale=pcol("nmp", kc)), reads=[Bp, B_const], writes=[B_hT])
                            else:
                                kb.op("dve", lambda e, kc=kc, i=i, pb=pb, tb=tb: e.tensor_scalar_mul(
                                    out=hT[:, kc, tb * 128:(tb + 1) * 128], in0=pb[:, i * 128:(i + 1) * 128],
                                    scalar1=pcol("nmp", kc)), reads=[Bp, B_const], writes=[B_hT])
                kb.barrier()
            ckpt("S0", [("hT", hT[:, :, :].rearrange("p a b -> p (a b)"), B_hT, BF16)])

            def hT_rhs(j, pr, kcs=None):
                return hT[:, j, :]

            def proj_in(c0, ncols, key):
                return ws.get(key, wsegs("w_in", 0, D, c0, ncols))

            def dense_h(wt, bw, cofs):
                acc, Ba = next_acc()
                if not null:
                    dense(acc, Ba, [(wt, bw, list(range(NB)), cofs, lambda j, pr: hT[:, j, :], [B_hT], None)])
                return acc, Ba

            with ExitStack() as sr:
                tw = sbt(sr, "tw", [128, TT], BF16)
                pa = sbt(sr, "pa", [128, TT], BF16)
                sg = sbt(sr, "sg", [128, 4, TT], BF16)
                B_small = Buf("small")
                raw = sbt(sr, "raw", [128, TT + 1], F32)
                B_raw = Buf("raw")
                dtmp = sbt(sr, "dtmp", [128, TT], F32)
                B_d = Buf("dtmp")
                XSETS = [(sbt(sr, "xs", [128, 6, TT], F32), [Buf("xs%d" % i) for i in range(6)]) for _ in range(2)]
                sm_xs = sbt(sr, "smxs", [128, TT], F32)
                B_smxs = Buf("smxs")

                def shift_block(acc, Ba, blk, dst, B_dst):
                    kb.op("act", lambda e: e.activation(out=raw[:, 1:TT + 1], in_=acc[:, 0:TT], func=AF.Copy),
                          reads=[Ba], writes=[B_raw])
                    kb.op("dve", lambda e: e.tensor_copy(out=raw[:, 0:1], in_=carry[:, blk:blk + 1]),
                          reads=[B_carry], writes=[B_raw])
                    kb.op("dve", lambda e: e.tensor_copy(out=carry[:, blk:blk + 1], in_=raw[:, TT:TT + 1]),
                          reads=[B_raw], writes=[B_carry])
                    kb.op("dve", lambda e: e.tensor_tensor(out=dtmp[:, :], in0=raw[:, 0:TT], in1=raw[:, 1:TT + 1],
                                                           op=ALU.subtract), reads=[B_raw], writes=[B_d])
                    kb.op("dve", lambda e: e.scalar_tensor_tensor(out=dst, in0=dtmp[:, :], scalar=pcol("mu", blk),
                                                                  in1=raw[:, 1:TT + 1], op0=ALU.mult, op1=ALU.add),
                          reads=[B_d, B_raw, B_const], writes=[B_dst])

                for ui, (c0, ncols) in enumerate([(RWS, 256), (RWS + 256, 256), (RWS + 512, 224)]):
                    wt, bw = proj_in(c0, ncols, ("small", t, ui))
                    for bi in range(2):
                        blk_small = ui * 2 + bi
                        rows = 128 if blk_small < 5 else 96
                        acc, Ba = dense_h(wt, bw, bi * 128)
                        shift_block(acc, Ba, 96 + blk_small, sm_xs[:, :], B_smxs)
                        if blk_small == 0:
                            kb.op("act", lambda e: e.activation(out=tw[:, :], in_=sm_xs[:, :], func=AF.Tanh),
                                  reads=[B_smxs], writes=[B_small])
                        elif blk_small == 1:
                            kb.op("act", lambda e: e.activation(out=pa[:, :], in_=sm_xs[:, :], func=AF.Copy),
                                  reads=[B_smxs], writes=[B_small])
                        else:
                            kb.op("act", lambda e, g=blk_small - 2: e.activation(out=sg[:, g, :], in_=sm_xs[:, :],
                                                                                 func=AF.Sigmoid),
                                  reads=[B_smxs], writes=[B_small])

                ckpt("S1", [("tw", tw[:, :], B_small, BF16), ("pa", pa[:, :], B_small, BF16),
                            ("sg", sg[:, :, :].rearrange("p a b -> p (a b)"), B_small, BF16)])
                def mkset():
                    lw = sbt(sr, "lw", [128, TT], F32)
                    av = sbt(sr, "av", [128, TT], F32)
                    gv = sbt(sr, "gv", [128, TT], F32)
                    kk = sbt(sr, "kk", [128, TT], F32)
                    k2 = sbt(sr, "k2", [128, TT], F32)
                    bv = sbt(sr, "bv", [128, TT], F32)
                    cs = sbt(sr, "cs", [128, TT], F32)
                    csx = sbt(sr, "csx", [128, TT], F32)
                    ex = sbt(sr, "ex", [128, TT], F32)
                    exb = sbt(sr, "exb", [128, NCH, 2, 64], F32)
                    t16 = sbt(sr, "t16", [128, TT], BF16)
                    bonus = sbt(sr, "bonus", [128, TT], F32)
                    nbias = sbt(sr, "nbias", [128, 2 * NCH], F32)
                    bdA = sbt(sr, "bdA", [128, NCH, 2, 64], BF16)
                    bdB = sbt(sr, "bdB", [128, NCH, 2, 64], BF16)
                    bdK = sbt(sr, "bdK", [128, NCH, 2, 64], BF16)
                    bdBc = sbt(sr, "bdBc", [128, NCH, 2, 64], BF16)
                    bdKc = sbt(sr, "bdKc", [128, NCH, 2, 64], BF16)
                    bdV = sbt(sr, "bdV", [128, NCH, 2, 64], BF16)
                    Rt = sbt(sr, "Rt", [128, TT], BF16)
                    Bp_ = {n: Buf(n) for n in ["lw", "av", "gv", "kk", "k2", "bv", "cs", "csx", "ex", "exb", "t16",
                                                "bonus", "nbias", "bdA", "bdB", "bdK", "bdBc", "bdKc", "bdV", "Rt"]}
                    m1s = [sbt(sr, "m1", [128, 512], BF16) for _ in range(NCH)]
                    B_m1s = [Buf("m1") for _ in range(NCH)]
                    pqs = [[sbt(sr, "pq", [128, 384], BF16) for i in range(2)] for _ in range(NCH)]
                    B_pqs = [[Buf("pq0"), Buf("pq1")] for _ in range(NCH)]
                    tms = [sbt(sr, "tm", [128, 384], BF16) for _ in range(2)]
                    B_tms = [Buf("tm0"), Buf("tm1")]
                    xu = sbt(sr, "xu", [128, 256], BF16)
                    B_xu = Buf("xu")
                    T16 = sbt(sr, "T16", [128, 128], BF16)
                    B_T16 = Buf("T16")
                    yT, B_yT = cs, Bp_["cs"]
                    pt1, B_pt1 = ex, Bp_["ex"]
                    pt2, B_pt2 = csx, Bp_["csx"]
                    return dict(locals())
                SETS = [mkset(), mkset()]

                def bcast_bd(ap):
                    return ap.rearrange("p (c t) -> p c t", t=64).unsqueeze(2).broadcast_to([128, NCH, 2, 64])

                def pair_body(sp_, bi, S, XS, wl, bwl, full):
                    xs, B_xs = XS
                    (lw, av, gv, kk, k2, bv, cs, csx, ex, exb, t16, bonus, nbias, bdA, bdB, bdK, bdBc, bdKc, bdV, Rt, Bp_, m1s, B_m1s, pqs, B_pqs, tms, B_tms, xu, B_xu, T16, B_T16, yT, B_yT, pt1, B_pt1, pt2, B_pt2) = [S[k_] for k_ in ['lw', 'av', 'gv', 'kk', 'k2', 'bv', 'cs', 'csx', 'ex', 'exb', 't16', 'bonus', 'nbias', 'bdA', 'bdB', 'bdK', 'bdBc', 'bdKc', 'bdV', 'Rt', 'Bp_', 'm1s', 'B_m1s', 'pqs', 'B_pqs', 'tms', 'B_tms', 'xu', 'B_xu', 'T16', 'B_T16', 'yT', 'B_yT', 'pt1', 'B_pt1', 'pt2', 'B_pt2']]
                    p = sp_ * 2 + bi
                    rs_, ks_, vs_ = xs[:, bi, :], xs[:, 2 + bi, :], xs[:, 4 + bi, :]
                    Brs, Bks, Bvs = B_xs[bi], B_xs[2 + bi], B_xs[4 + bi]
                    c0 = bi * 128
                    pb, Bp = next_rot()
                    yield kb.op("pe", lambda e, pb=pb, c0=c0: e.matmul(pb[:, 0:TT], lhsT=wl[:, 0, c0:c0 + 128],
                                                                 rhs=tw[:, :], start=True, stop=True),
                          reads=[bwl, B_small], writes=[Bp])
                    yield kb.op("act", lambda e, pb=pb, p=p: e.activation(out=lw[:, :], in_=pb[:, 0:TT], func=AF.Sigmoid,
                                                                    bias=pcol("w0", p)),
                          reads=[Bp, B_const], writes=[Bp_["lw"]])
                    pb, Bp = next_rot()
                    yield kb.op("pe", lambda e, pb=pb, c0=c0: e.matmul(pb[:, 0:TT], lhsT=wl[:, 1, c0:c0 + 128],
                                                                 rhs=pa[:, :], start=True, stop=True),
                          reads=[bwl, B_small], writes=[Bp])
                    yield kb.op("act", lambda e, pb=pb, p=p: e.activation(out=av[:, :], in_=pb[:, 0:TT], func=AF.Sigmoid,
                                                                    bias=pcol("a0", p)),
                          reads=[Bp, B_const], writes=[Bp_["av"]])
                    if full:
                        pb, Bp = next_rot()

                        def fng(e, pb=pb, c0=c0):
                            inst = None
                            for g in range(4):
                                pr = 128 if g < 3 else 96
                                inst = e.matmul(pb[:, 0:TT], lhsT=wl[0:pr, 2 + g, c0:c0 + 128], rhs=sg[0:pr, g, :],
                                                start=(g == 0), stop=(g == 3))
                            return inst
                        yield kb.op("pe", fng, reads=[bwl, B_small], writes=[Bp])
                        yield kb.op("act", lambda e, pb=pb: e.activation(out=gv[:, :], in_=pb[:, 0:TT], func=AF.Copy),
                                    reads=[Bp], writes=[Bp_["gv"]])
                    else:
                        yield None
                        yield None
                    yield kb.op("dve", lambda e, p=p, ks_=ks_: e.tensor_scalar_mul(out=kk[:, :], in0=ks_, scalar1=pcol("kk", p)),
                          reads=[Bks, B_const], writes=[Bp_["kk"]])
                    yield kb.op("act", lambda e: e.activation(out=t16[:, :], in_=kk[:, :], func=AF.Square),
                          reads=[Bp_["kk"]], writes=[Bp_["t16"]])
                    pb, Bp = next_rot()
                    yield kb.op("pe", lambda e, pb=pb: e.matmul(pb[:, 0:TT], lhsT=bdones_b, rhs=t16[:, :], start=True,
                                                          stop=True), reads=[Bp_["t16"], B_const], writes=[Bp])
                    yield kb.op("dve", lambda e, pb=pb: e.tensor_scalar_max(out=ex[:, :], in0=pb[:, 0:TT], scalar1=1e-24),
                          reads=[Bp], writes=[Bp_["ex"]])
                    yield kb.op("act", lambda e: e.activation(out=ex[:, :], in_=ex[:, :], func=AF.Sqrt),
                          reads=[Bp_["ex"]], writes=[Bp_["ex"]])
                    yield kb.op("dve", lambda e: e.reciprocal(out=ex[:, :], in_=ex[:, :]), reads=[Bp_["ex"]],
                          writes=[Bp_["ex"]])
                    yield kb.op("dve", lambda e: e.tensor_tensor(out=kk[:, :], in0=kk[:, :], in1=ex[:, :], op=ALU.mult),
                          reads=[Bp_["ex"], Bp_["kk"]], writes=[Bp_["kk"]])
                    yield kb.op("dve", lambda e, p=p: e.tensor_scalar(out=k2[:, :], in0=av[:, :], scalar1=pcol("ka", p),
                                                                scalar2=pcol("omka", p), op0=ALU.mult, op1=ALU.add),
                          reads=[Bp_["av"], B_const], writes=[Bp_["k2"]])
                    yield kb.op("dve", lambda e, ks_=ks_: e.tensor_tensor(out=k2[:, :], in0=k2[:, :], in1=ks_, op=ALU.mult),
                          reads=[Bks, Bp_["k2"]], writes=[Bp_["k2"]])
                    yield kb.op("dve", lambda e: e.tensor_tensor(out=bv[:, :], in0=kk[:, :], in1=av[:, :], op=ALU.mult),
                          reads=[Bp_["kk"], Bp_["av"]], writes=[Bp_["bv"]])
                    if full:
                        yield kb.op("dve", lambda e, rs_=rs_: e.tensor_tensor(out=ex[:, :], in0=rs_, in1=k2[:, :], op=ALU.mult),
                                    reads=[Brs, Bp_["k2"]], writes=[Bp_["ex"]])
                        yield kb.op("act", lambda e, p=p: e.activation(out=t16[:, :], in_=ex[:, :], func=AF.Identity,
                                                                       scale=pcol("rk", p)),
                                    reads=[Bp_["ex"], B_const], writes=[Bp_["t16"]])
                        pb, Bp = next_rot()
                        yield kb.op("pe", lambda e, pb=pb: e.matmul(pb[:, 0:TT], lhsT=bdones_b, rhs=t16[:, :], start=True,
                                                                    stop=True), reads=[Bp_["t16"], B_const], writes=[Bp])
                        yield kb.op("dve", lambda e, pb=pb, vs_=vs_: e.tensor_tensor(out=bonus[:, :], in0=pb[:, 0:TT], in1=vs_,
                                                                                     op=ALU.mult),
                                    reads=[Bp, Bvs], writes=[Bp_["bonus"]])
                    yield kb.op("dve", lambda e: e.tensor_tensor_scan(out=cs[:, :], data0=segmask, data1=lw[:, :],
                                                                initial=0.0, op0=ALU.mult, op1=ALU.add),
                          reads=[Bp_["lw"], B_const], writes=[Bp_["cs"]])
                    yield kb.op("dve", lambda e: e.tensor_tensor(out=csx[:, :], in0=cs[:, :], in1=lw[:, :], op=ALU.subtract),
                          reads=[Bp_["cs"], Bp_["lw"]], writes=[Bp_["csx"]])
                    csC = cs[:, :].rearrange("p (c t) -> p c t", t=64)[:, :, 63:64].rearrange("p c o -> p (c o)")
                    yield kb.op("dve", lambda e: e.tensor_scalar_mul(out=nbias[:, 0:NCH], in0=csC, scalar1=-KAPPA),
                          reads=[Bp_["cs"]], writes=[Bp_["nbias"]])
                    yield kb.op("act", lambda e: e.activation(out=nbias[:, NCH:2 * NCH], in_=nbias[:, 0:NCH], func=AF.Exp),
                          reads=[Bp_["nbias"]], writes=[Bp_["nbias"]])
                    if full:
                        yield kb.op("act", lambda e: e.activation(out=ex[:, :], in_=cs[:, :], func=AF.Exp, scale=-KAPPA),
                                    reads=[Bp_["cs"]], writes=[Bp_["ex"]])
                        yield kb.op("dve", lambda e, rs_=rs_: e.tensor_tensor(out=Rt[:, :], in0=rs_, in1=ex[:, :], op=ALU.mult),
                                    reads=[Brs, Bp_["ex"]], writes=[Bp_["Rt"]])
                    yield kb.op("act", lambda e: e.activation(out=ex[:, :], in_=csx[:, :], func=AF.Exp, scale=-KAPPA),
                          reads=[Bp_["csx"]], writes=[Bp_["ex"]])
                    yield kb.op("dve", lambda e: e.scalar_tensor_tensor(out=ex[:, :], in0=kk[:, :], scalar=-1.0,
                                                                  in1=ex[:, :], op0=ALU.mult, op1=ALU.mult),
                          reads=[Bp_["kk"], Bp_["ex"]], writes=[Bp_["ex"]])
                    yield kb.op("dve", lambda e: e.tensor_tensor(out=bdA[:, :, :, :], in0=bcast_bd(ex[:, :]), in1=bdm_b,
                                                           op=ALU.mult),
                          reads=[Bp_["ex"], B_const], writes=[Bp_["bdA"]])
                    yield kb.op("act", lambda e: e.activation(out=ex[:, :], in_=cs[:, :], func=AF.Exp, scale=KAPPA),
                          reads=[Bp_["cs"]], writes=[Bp_["ex"]])
                    yield kb.op("dve", lambda e: e.tensor_tensor(out=exb[:, :, :, :], in0=bcast_bd(ex[:, :]), in1=bdm_b,
                                                           op=ALU.mult),
                          reads=[Bp_["ex"], B_const], writes=[Bp_["exb"]])
                    yield kb.op("dve", lambda e: e.tensor_tensor(out=bdB[:, :, :, :], in0=bcast_bd(bv[:, :]),
                                                           in1=exb[:, :, :, :], op=ALU.mult),
                          reads=[Bp_["bv"], Bp_["exb"]], writes=[Bp_["bdB"]])
                    yield kb.op("dve", lambda e: e.tensor_tensor(out=bdK[:, :, :, :], in0=bcast_bd(k2[:, :]),
                                                           in1=exb[:, :, :, :], op=ALU.mult),
                          reads=[Bp_["k2"], Bp_["exb"]], writes=[Bp_["bdK"]])
                    for c in range(NCH):
                        yield kb.op("act", lambda e, c=c: e.activation(out=ex[:, c * 64:(c + 1) * 64],
                                                                 in_=cs[:, c * 64:(c + 1) * 64], func=AF.Exp,
                                                                 scale=KAPPA, bias=nbias[:, c:c + 1]),
                              reads=[Bp_["cs"], Bp_["nbias"]], writes=[Bp_["ex"]])
                    yield kb.op("dve", lambda e: e.tensor_tensor(out=exb[:, :, :, :], in0=bcast_bd(ex[:, :]), in1=bdm_b,
                                                           op=ALU.mult),
                          reads=[Bp_["ex"], B_const], writes=[Bp_["exb"]])
                    yield kb.op("dve", lambda e: e.tensor_tensor(out=bdBc[:, :, :, :], in0=bcast_bd(bv[:, :]),
                                                           in1=exb[:, :, :, :], op=ALU.mult),
                          reads=[Bp_["bv"], Bp_["exb"]], writes=[Bp_["bdBc"]])
                    yield kb.op("dve", lambda e: e.tensor_tensor(out=bdKc[:, :, :, :], in0=bcast_bd(k2[:, :]),
                                                           in1=exb[:, :, :, :], op=ALU.mult),
                          reads=[Bp_["k2"], Bp_["exb"]], writes=[Bp_["bdKc"]])
                    yield kb.op("dve", lambda e, vs_=vs_: e.tensor_tensor(out=bdV[:, :, :, :], in0=bcast_bd(vs_), in1=bdm_b,
                                                                    op=ALU.mult),
                          reads=[Bvs, B_const], writes=[Bp_["bdV"]])
                    yield kb.op("act", lambda e, p=p: e.activation(out=T16[:, :], in_=st32[:, p, :], func=AF.Copy),
                          reads=[B_st[p]], writes=[B_T16])
                    yb, Byb = pbank[YB], B_pb[YB]

                    def bdc(tl, c):
                        return tl[:, c, :, :].rearrange("p h t -> p (h t)")
                    for c in range(NCH):
                        At_c, Bt_c, Kt_c = bdc(bdA, c), bdc(bdB, c), bdc(bdK, c)
                        Rt_c = Rt[:, c * 64:(c + 1) * 64]
                        pb, Bp = next_rot()

                        def fn1(e, pb=pb, At_c=At_c, Bt_c=Bt_c, Kt_c=Kt_c, Rt_c=Rt_c):
                            e.matmul(pb[:, 0:128], lhsT=Bt_c, rhs=At_c, start=True, stop=True)
                            e.matmul(pb[:, 128:256], lhsT=Kt_c, rhs=At_c, start=True, stop=True)
                            inst = e.matmul(pb[:, 256:384], lhsT=At_c, rhs=Bt_c, start=True, stop=True)
                            if not full:
                                return inst
                            e.matmul(pb[:, 384:448], lhsT=Bt_c, rhs=Rt_c, start=True, stop=True)
                            return e.matmul(pb[:, 448:512], lhsT=Kt_c, rhs=Rt_c, start=True, stop=True)
                        yield kb.op("pe", fn1, reads=[Bp_["bdA"], Bp_["bdB"], Bp_["bdK"]] + ([Bp_["Rt"]] if full else []),
                                    writes=[Bp])
                        mhi = 512 if full else 384
                        yield kb.op("dve", lambda e, pb=pb, mhi=mhi, c=c: e.tensor_tensor(
                            out=m1s[c][:, 0:mhi], in0=pb[:, 0:mhi], in1=mcat_b[:, 0:mhi], op=ALU.mult),
                            reads=[Bp, B_const], writes=[B_m1s[c]])
                    cur = [(m1s[c][:, 256:384], m1s[c][:, 0:128], ident_b, [B_m1s[c], B_const]) for c in range(NCH)]
                    for lvl in range(6):
                        for c in range(NCH):
                            curP, curQ, curM, curB = cur[c]
                            pb, Bp = next_rot()
                            dst = pqs[c][lvl % 2]
                            Bdst = B_pqs[c][lvl % 2]

                            def fnl(e, pb=pb, curP=curP, curQ=curQ, curM=curM, lvl=lvl):
                                if lvl < 5:
                                    e.matmul(pb[:, 0:128], lhsT=curQ, rhs=curP, start=True, stop=True)
                                    e.matmul(pb[:, 128:256], lhsT=curP, rhs=curQ, start=True, stop=True)
                                e.matmul(pb[:, 256:384], lhsT=curP, rhs=curM, start=True, stop=False)
                                return e.matmul(pb[:, 256:384], lhsT=ident_b, rhs=curM, start=False, stop=True)
                            yield kb.op("pe", fnl, reads=curB, writes=[Bp])
                            lo = 0 if lvl < 5 else 256
                            if (lvl + c) % 2:
                                yield kb.op("act", lambda e, pb=pb, dst=dst, lo=lo: e.activation(
                                    out=dst[:, lo:384], in_=pb[:, lo:384], func=AF.Copy), reads=[Bp], writes=[Bdst])
                            else:
                                yield kb.op("dve", lambda e, pb=pb, dst=dst, lo=lo: e.tensor_copy(
                                    out=dst[:, lo:384], in_=pb[:, lo:384]), reads=[Bp], writes=[Bdst])
                            cur[c] = (dst[:, 0:128], dst[:, 128:256], dst[:, 256:384], [Bdst, B_const])

                    def transposes(c):
                        ptile, Bpt = next_pt()
                        tm, B_tm = tms[c % 2], B_tms[c % 2]

                        def fnt(e, ptile=ptile, c=c):
                            inst = None
                            for i, src in enumerate([bdBc, bdKc, bdV]):
                                inst = e.transpose(ptile[:, i * 128:(i + 1) * 128], bdc(src, c), ident_b)
                            return inst
                        t1_ = kb.op("pe", fnt, reads=[Bp_["bdBc"], Bp_["bdKc"], Bp_["bdV"], B_const], writes=[Bpt])
                        t2_ = kb.op("act", lambda e, ptile=ptile, tm=tm: e.activation(out=tm[:, :], in_=ptile[:, 0:384],
                                                                                    func=AF.Copy), reads=[Bpt], writes=[B_tm])
                        return t1_, t2_
                    transposes(0)
                    yield None
                    for c in range(NCH):
                        if c + 1 < NCH:
                            transposes(c + 1)
                            yield None
                        tm, B_tm = tms[c % 2], B_tms[c % 2]
                        At_c = bdc(bdA, c)
                        Rt_c = Rt[:, c * 64:(c + 1) * 64]
                        m1, B_m1 = m1s[c], B_m1s[c]
                        LakT, Mrb, Mrk = m1[:, 128:256], m1[:, 384:448], m1[:, 448:512]
                        WT, B_WT = cur[c][2], cur[c][3][0]
                        pb, Bp = next_rot()

                        def fnx(e, pb=pb, At_c=At_c, LakT=LakT, tm=tm):
                            e.matmul(pb[:, 0:128], lhsT=At_c, rhs=T16[:, :], start=True, stop=False)
                            return e.matmul(pb[:, 0:128], lhsT=LakT, rhs=tm[:, 256:384], start=False, stop=True)
                        yield kb.op("pe", fnx, reads=[Bp_["bdA"], B_T16, B_m1, B_tm], writes=[Bp])
                        yield kb.op("act", lambda e, pb=pb: e.activation(out=xu[:, 0:128], in_=pb[:, 0:128], func=AF.Copy),
                                    reads=[Bp], writes=[B_xu])
                        pb, Bp = next_rot()
                        yield kb.op("pe", lambda e, pb=pb, WT=WT: e.matmul(pb[:, 0:128], lhsT=WT, rhs=xu[:, 0:128],
                                                                           start=True, stop=True),
                                    reads=[B_WT, B_xu], writes=[Bp])
                        yield kb.op("dve", lambda e, pb=pb: e.tensor_copy(out=xu[:, 128:256], in_=pb[:, 0:128]),
                                    reads=[Bp], writes=[B_xu])

                        def fny(e, c=c, Rt_c=Rt_c, Mrb=Mrb, Mrk=Mrk, tm=tm):
                            o = yb[:, bi * TT + c * 64:bi * TT + (c + 1) * 64]
                            e.matmul(o, lhsT=T16[:, :], rhs=Rt_c, start=True, stop=False)
                            e.matmul(o, lhsT=xu[:, 128:256], rhs=Mrb, start=False, stop=False)
                            return e.matmul(o, lhsT=tm[:, 256:384], rhs=Mrk, start=False, stop=True)
                        if full:
                            yield kb.op("pe", fny, reads=[B_T16, Bp_["Rt"], B_xu, B_m1, B_tm], writes=[Byb])
                        pb, Bp = next_rot()

                        def fns(e, pb=pb, tm=tm):
                            e.matmul(pb[:, 0:128], lhsT=tm[:, 0:128], rhs=xu[:, 128:256], start=True, stop=False)
                            return e.matmul(pb[:, 0:128], lhsT=tm[:, 128:256], rhs=tm[:, 256:384], start=False,
                                            stop=True)
                        yield kb.op("pe", fns, reads=[B_tm, B_xu], writes=[Bp])
                        yield kb.op("dve", lambda e, pb=pb, p=p, c=c: e.scalar_tensor_tensor(
                            out=st32[:, p, :], in0=st32[:, p, :], scalar=nbias[:, NCH + c:NCH + c + 1],
                            in1=pb[:, 0:128], op0=ALU.mult, op1=ALU.add),
                            reads=[Bp, Bp_["nbias"], B_st[p]], writes=[B_st[p]])
                        yield kb.op("act", lambda e, p=p: e.activation(out=T16[:, :], in_=st32[:, p, :], func=AF.Copy),
                                    reads=[B_st[p]], writes=[B_T16])
                    if not full:
                        return
                    yield kb.op("act", lambda e: e.activation(out=yT[:, :], in_=yb[:, bi * TT:(bi + 1) * TT], func=AF.Copy), reads=[Byb],
                          writes=[B_yT])
                    yield kb.op("act", lambda e: e.activation(out=t16[:, :], in_=yT[:, :], func=AF.Copy), reads=[B_yT],
                          writes=[Bp_["t16"]])
                    pb, Bp = next_rot()
                    yield kb.op("pe", lambda e, pb=pb: e.matmul(pb[:, 0:TT], lhsT=bdones_b, rhs=t16[:, :], start=True,
                                                          stop=True), reads=[Bp_["t16"], B_const], writes=[Bp])
                    yield kb.op("dve", lambda e, pb=pb: e.scalar_tensor_tensor(out=pt1[:, :], in0=pb[:, 0:TT],
                                                                         scalar=-1.0 / 64, in1=yT[:, :],
                                                                         op0=ALU.mult, op1=ALU.add),
                          reads=[Bp, B_yT], writes=[B_pt1])
                    yield kb.op("act", lambda e: e.activation(out=t16[:, :], in_=pt1[:, :], func=AF.Square),
                          reads=[B_pt1], writes=[Bp_["t16"]])
                    pb, Bp = next_rot()
                    yield kb.op("pe", lambda e, pb=pb: e.matmul(pb[:, 0:TT], lhsT=bdones_b, rhs=t16[:, :], start=True,
                                                          stop=True), reads=[Bp_["t16"], B_const], writes=[Bp])
                    yield kb.op("dve", lambda e, pb=pb: e.tensor_scalar(out=pt2[:, :], in0=pb[:, 0:TT], scalar1=1.0 / 64,
                                                                  scalar2=64e-5, op0=ALU.mult, op1=ALU.add),
                          reads=[Bp], writes=[B_pt2])
                    yield kb.op("act", lambda e: e.activation(out=pt2[:, :], in_=pt2[:, :], func=AF.Sqrt), reads=[B_pt2],
                          writes=[B_pt2])
                    yield kb.op("dve", lambda e: e.reciprocal(out=pt2[:, :], in_=pt2[:, :]), reads=[B_pt2], writes=[B_pt2])
                    yield kb.op("dve", lambda e: e.tensor_tensor(out=pt1[:, :], in0=pt1[:, :], in1=pt2[:, :], op=ALU.mult),
                          reads=[B_pt1, B_pt2], writes=[B_pt1])
                    yield kb.op("dve", lambda e, p=p: e.tensor_scalar(out=pt1[:, :], in0=pt1[:, :], scalar1=pcol("lnw", p),
                                                                scalar2=pcol("lnb", p), op0=ALU.mult, op1=ALU.add),
                          reads=[B_pt1, B_const], writes=[B_pt1])
                    yield kb.op("dve", lambda e: e.tensor_tensor(out=pt1[:, :], in0=pt1[:, :], in1=bonus[:, :], op=ALU.add),
                          reads=[B_pt1, Bp_["bonus"]], writes=[B_pt1])
                    yield kb.op("dve", lambda e, p=p: e.tensor_tensor(out=orw[:, p, :], in0=pt1[:, :], in1=gv[:, :],
                                                                op=ALU.mult),
                          reads=[B_pt1, Bp_["gv"]], writes=[B_orw])

                def dense_gen(sp_, XS):
                    xs_, Bxs_ = XS
                    for wi, base in enumerate([RW0, RW0 + 4096, RW0 + 8192]):
                        if wi == 0 and not (is_main or lastpre):
                            continue
                        wt, bw = proj_in(base + sp_ * 256, 256, ("rkv", t, sp_, wi))
                        for bi in range(2):
                            acc, Ba = next_acc()
                            for q_ in range(4):
                                def fnq(e, acc=acc, wt=wt, bi=bi, q_=q_):
                                    inst = None
                                    for kc in range(q_ * 8, q_ * 8 + 8):
                                        inst = e.matmul(acc[:, 0:TT], lhsT=wt[:, kc, bi * 128:(bi + 1) * 128], rhs=hT[:, kc, :],
                                                        start=(kc == 0), stop=(kc == NB - 1))
                                    return inst
                                kb.op("pe", fnq, reads=[bw, B_hT], writes=[Ba])
                                yield None
                            blk = wi * 32 + sp_ * 2 + bi
                            shift_block(acc, Ba, blk, xs_[:, wi * 2 + bi, :], Bxs_[wi * 2 + bi])
                            yield None

                for _ in dense_gen(0, XSETS[0]):
                    pass
                for sp_ in range(16):
                    segs = (wsegs("w2", 0, 128, sp_ * 256, 256, kc0=0) + wsegs("a2", 0, 128, sp_ * 256, 256, kc0=1)
                            + wsegs("g2", 0, 480, sp_ * 256, 256, kc0=2))
                    wl, bwl = ws.get(("lora", t, sp_), segs)
                    nxt = dense_gen(sp_ + 1, XSETS[(sp_ + 1) % 2]) if sp_ < 15 else None
                    if null:
                        if nxt is not None:
                            for _ in nxt:
                                pass
                        continue
                    gens = [pair_body(sp_, bi, SETS[bi], XSETS[sp_ % 2], wl, bwl, is_main) for bi in range(2)]
                    for g_ in gens:
                        for _ in range(6):
                            next(g_)
                    rnd = 0
                    while gens or nxt is not None:
                        for g_ in list(gens):
                            try:
                                next(g_)
                            except StopIteration:
                                gens.remove(g_)
                        rnd += 1
                        if nxt is not None and (rnd % 4 == 0 or not gens):
                            try:
                                next(nxt)
                            except StopIteration:
                                nxt = None
                kb.barrier()
                ckpt("RWKV", [("orw", orw[:, :, :].rearrange("p a b -> p (a b)"), B_orw, BF16)])

            if not is_main:
                mute(True)
            gm = sbt(sm, "gm", [128, NB, TT], BF16)
            B_gm = Buf("gm")
            oat = sbt(sm, "oat", [128, NB, TT], BF16)
            B_oat = Buf("oat")
            with ExitStack() as sg_:
                sig = sbt(sg_, "sig", [128, TT], F32)
                B_sig = Buf("sig")
                brt = sbt(sg_, "brt", [128, 2, TT], F32)
                B_brt = [Buf("brt0"), Buf("brt1")]
                for u in range(16):
                    wtb, bwb = ws.get(("wrb", t, u), wsegs("wrb", 0, D, u * 256, 256))
                    if not null:
                        for bi in range(2):
                            acc1, Ba1 = next_acc()
                            dense(acc1, Ba1, [(wtb, bwb, list(range(NB)), bi * 128, lambda k, pr: orw[:, k, :], [B_orw], None)])
                            kb.op("dve", lambda e, acc1=acc1, bi=bi: e.tensor_copy(out=brt[:, bi, :], in_=acc1[:, 0:TT]),
                                  reads=[Ba1], writes=[B_brt[bi]])
                    wtg, bwg = proj_in(GATE0 + D + u * 256, 256, ("grw", t, u))
                    if null:
                        continue
                    for bi in range(2):
                        j = u * 2 + bi
                        acc2, Ba2 = next_acc()
                        dense(acc2, Ba2, [(wtg, bwg, list(range(NB)), bi * 128, lambda k, pr: hT[:, k, :], [B_hT], None)])
                        kb.op("act", lambda e, acc2=acc2: e.activation(out=sig[:, :], in_=acc2[:, 0:TT], func=AF.Sigmoid),
                              reads=[Ba2], writes=[B_sig])
                        kb.op("dve", lambda e, j=j, bi=bi: e.tensor_tensor(out=gm[:, j, :], in0=brt[:, bi, :],
                                                                           in1=sig[:, :], op=ALU.mult),
                              reads=[B_brt[bi], B_sig], writes=[B_gm])
                kb.barrier()

            ckpt("GM", [("gm", gm[:, :, :].rearrange("p a b -> p (a b)"), B_gm, BF16)])
            with ExitStack() as sa:
                cosT = sbt(sa, "cosT", [128, TT], F32)
                sinT = sbt(sa, "sinT", [128, TT], F32)
                B_rope = Buf("rope")
                qb16 = sbt(sa, "qb16", [128, TT], BF16)
                B_qb = Buf("qb16")
                r1 = sbt(sa, "r1", [128, TT], F32)
                r2 = sbt(sa, "r2", [128, TT], F32)
                B_r1, B_r2 = Buf("r1"), Buf("r2")
                qT = sbt(sa, "qT", [128, 4, TT], BF16)
                B_qT = Buf("qT")
                qsq = sbt(sa, "qsq", [128, 512], BF16)
                B_qsq = Buf("qsq")
                nm = sbt(sa, "nm", [128, 512], BF16)
                B_nm = Buf("nm")
                nk = sbt(sa, "nk", [128, 2], F32)
                B_nk = Buf("nk")
                pT = sbt(sa, "pT", [128, 2, 512], BF16)
                B_pT = Buf("pT")
                psk = sbt(sa, "psk", [128, 512], BF16)
                B_psk = Buf("psk")
                rden = sbt(sa, "rden", [128, 512], F32)
                B_rden = Buf("rden")
                ksq = sbt(sa, "ksq", [128, 128], BF16)
                B_ksq = Buf("ksq")
                if lastpre:
                    mute(False)
                kb.dma("sp", cosT[:, :], T_["ropec"][:, tok0:tok0 + TT], writes=[B_rope])
                kb.dma("sp", sinT[:, :], T_["ropes"][:, tok0:tok0 + TT], writes=[B_rope])

                def rope_block(acc, Ba, bias_col, dst, B_dst):
                    kb.op("act", lambda e: e.activation(out=qb16[:, :], in_=acc[:, 0:TT], func=AF.Identity,
                                                        bias=bias_col), reads=[Ba, B_const], writes=[B_qb])
                    pb, Bp = next_rot()
                    kb.op("pe", lambda e: e.matmul(pb[:, 0:TT], lhsT=psw_b, rhs=qb16[:, :], start=True, stop=True),
                          reads=[B_qb, B_const], writes=[Bp])
                    kb.op("dve", lambda e: e.tensor_tensor(out=r1[:, :], in0=qb16[:, :], in1=cosT[:, :], op=ALU.mult),
                          reads=[B_qb, B_rope], writes=[B_r1])
                    kb.op("dve", lambda e: e.tensor_tensor(out=r2[:, :], in0=pb[:, 0:TT], in1=sinT[:, :], op=ALU.mult),
                          reads=[Bp, B_rope], writes=[B_r2])
                    kb.op("dve", lambda e: e.tensor_tensor(out=dst, in0=r1[:, :], in1=r2[:, :], op=ALU.add),
                          reads=[B_r1, B_r2], writes=[B_dst])

                for u in range(4):
                    wt, bw = proj_in(4096 + u * 256, 256, ("attk", t, u))
                    if null:
                        continue
                    for bi in range(2):
                        g = u * 2 + bi
                        acc, Ba = dense_h(wt, bw, bi * 128)
                        rope_block(acc, Ba, pcol("bqkv", 32 + g), kT[:, g, 128:128 + TT], B_kT)
                        for qb_ in range(NQB):
                            kb.op("act", lambda e, g=g, qb_=qb_: e.activation(
                                out=ksq[:, :], in_=kT[:, g, 128 + qb_ * 128:256 + qb_ * 128], func=AF.Square),
                                reads=[B_kT], writes=[B_ksq])
                            pb, Bp = next_rot()
                            kb.op("pe", lambda e, pb=pb: e.matmul(pb[:, 0:128], lhsT=ones_b, rhs=ksq[:, :], start=True,
                                                                  stop=True), reads=[B_ksq, B_const], writes=[Bp])
                            kb.op("dve", lambda e, pb=pb, g=g, qb_=qb_: e.reduce_max(
                                out=kmx[:, 1 + qb_, g:g + 1], in_=pb[:, 0:128], axis=mybir.AxisListType.X),
                                reads=[Bp], writes=[B_kmx])
                for u in range(4):
                    wt, bw = proj_in(5120 + u * 256, 256, ("attv", t, u))
                    if null:
                        continue
                    for bi in range(2):
                        g = u * 2 + bi
                        acc, Ba = dense_h(wt, bw, bi * 128)
                        kb.op("act", lambda e, acc=acc, g=g: e.activation(out=qb16[:, :], in_=acc[:, 0:TT],
                                                                          func=AF.Identity, bias=pcol("bqkv", 40 + g)),
                              reads=[Ba, B_const], writes=[B_qb])
                        ptile, Bpt = next_pt()

                        def fnv(e, ptile=ptile):
                            inst = None
                            for qb_ in range(NQB):
                                inst = e.transpose(ptile[:, qb_ * 128:(qb_ + 1) * 128],
                                                   qb16[:, qb_ * 128:(qb_ + 1) * 128], ident_b)
                            return inst
                        kb.op("pe", fnv, reads=[B_qb, B_const], writes=[Bpt])
                        kb.op("dve", lambda e, ptile=ptile, g=g: e.tensor_copy(
                            out=vtok[:, 1:1 + NQB, g, :], in_=ptile[:, 0:NQB * 128].rearrange("p (b d) -> p b d", d=128)),
                            reads=[Bpt], writes=[B_vtok])
                if not is_main:
                    mute(True)
                for g in range(8):
                    for u2 in range(2):
                        u = g * 2 + u2
                        wt, bw = proj_in(u * 256, 256, ("attq", t, u))
                        if null:
                            continue
                        for bi in range(2):
                            hq = u * 2 + bi
                            acc, Ba = dense_h(wt, bw, bi * 128)
                            rope_block(acc, Ba, pcol("bqkv", hq), qT[:, u2 * 2 + bi, :], B_qT)
                    if null:
                        continue
                    for qb_ in range(NQB):
                        has_prev = not (t == 0 and qb_ == 0)
                        first_main = (NPRE > 0 and t == NPRE and qb_ == 0)
                        qg = qT[:, :, qb_ * 128:(qb_ + 1) * 128]
                        kb.op("act", lambda e, qg=qg: e.activation(out=qsq[:, :].rearrange("p (h q) -> p h q", h=4),
                                                                   in_=qg, func=AF.Square), reads=[B_qT],
                              writes=[B_qsq])
                        pb, Bp = next_rot()
                        kb.op("pe", lambda e, pb=pb: e.matmul(pb[:, :], lhsT=ones_b, rhs=qsq[:, :], start=True, stop=True),
                              reads=[B_qsq, B_const], writes=[Bp])
                        if has_prev:
                            kb.op("dve", lambda e, g=g, qb_=qb_: e.tensor_tensor(out=nk[:, 0:1], in0=kmx[:, qb_, g:g + 1],
                                                                                 in1=kmx[:, qb_ + 1, g:g + 1], op=ALU.max),
                                  reads=[B_kmx], writes=[B_nk])
                        else:
                            kb.op("dve", lambda e, g=g, qb_=qb_: e.tensor_copy(out=nk[:, 0:1], in_=kmx[:, qb_ + 1, g:g + 1]),
                                  reads=[B_kmx], writes=[B_nk])
                        kb.op("dve", lambda e: e.tensor_scalar_mul(out=nk[:, 1:2], in0=nk[:, 0:1], scalar1=-0.5),
                              reads=[B_nk], writes=[B_nk])
                        kb.op("dve", lambda e, pb=pb: e.tensor_scalar(out=nm[:, :], in0=pb[:, :], scalar1=-0.5,
                                                                      scalar2=nk[:, 1:2], op0=ALU.mult, op1=ALU.add),
                              reads=[Bp, B_nk], writes=[B_nm])
                        for h in range(4):
                            kb.op("act", lambda e, h=h, g=g: e.activation(
                                out=psk[:, h * 128:(h + 1) * 128], in_=nm[:, h * 128:(h + 1) * 128], func=AF.Exp,
                                scale=SCALE, bias=pcol("sink", g * 4 + h)), reads=[B_nm, B_const], writes=[B_psk])
                        blocks = ([0] if has_prev else []) + [1]
                        for kbk in blocks:
                            pb, Bp = next_rot()
                            keys = kT[:, g, qb_ * 128 + kbk * 128: qb_ * 128 + kbk * 128 + 128]
                            mk = (mprev0_b if first_main else mprev_b) if kbk == 0 else mcur_b

                            def fnsc(e, pb=pb, keys=keys, mk=mk, qg=qg):
                                e.matmul(pb[:, :].rearrange("p (h q) -> p h q", h=4), lhsT=keys, rhs=qg, start=True, stop=False)
                                e.matmul(pb[:, :], lhsT=ones_b[0:1, :], rhs=nm[0:1, :], start=False, stop=False)
                                return e.matmul(pb[:, :], lhsT=ident_b, rhs=mk, start=False, stop=True)
                            kb.op("pe", fnsc, reads=[B_kT, B_qT, B_nm, B_const], writes=[Bp])
                            kb.op("act", lambda e, pb=pb, kbk=kbk: e.activation(out=pT[:, kbk, :], in_=pb[:, :],
                                                                                func=AF.Exp, scale=SCALE),
                                  reads=[Bp], writes=[B_pT])
                        pbo, Bpo = next_rot()
                        pbd, Bpd = next_rot()

                        def fnpv(e, pbo=pbo, blocks=blocks, g=g, qb_=qb_):
                            inst = None
                            for i, kbk in enumerate(blocks):
                                inst = e.matmul(pbo[:, :], lhsT=vtok[:, qb_ + kbk, g, :], rhs=pT[:, kbk, :],
                                                start=(i == 0), stop=(i == len(blocks) - 1))
                            return inst
                        kb.op("pe", fnpv, reads=[B_vtok, B_pT], writes=[Bpo])

                        def fnden(e, pbd=pbd, blocks=blocks):
                            for i, kbk in enumerate(blocks):
                                e.matmul(pbd[:, :], lhsT=ones_b, rhs=pT[:, kbk, :], start=(i == 0), stop=False)
                            return e.matmul(pbd[:, :], lhsT=ident_b, rhs=psk[:, :], start=False, stop=True)
                        kb.op("pe", fnden, reads=[B_pT, B_psk, B_const], writes=[Bpd])
                        kb.op("dve", lambda e, pbd=pbd: e.reciprocal(out=rden[:, :], in_=pbd[:, :]), reads=[Bpd],
                              writes=[B_rden])
                        kb.op("dve", lambda e, pbo=pbo, g=g, qb_=qb_: e.tensor_tensor(
                            out=oat[:, g * 4:(g + 1) * 4, qb_ * 128:(qb_ + 1) * 128],
                            in0=pbo[:, :].rearrange("p (h q) -> p h q", h=4),
                            in1=rden[:, :].rearrange("p (h q) -> p h q", h=4), op=ALU.mult),
                            reads=[Bpo, B_rden], writes=[B_oat])
                if lastpre:
                    mute(False)
                kb.op("dve", lambda e: e.tensor_copy(out=kT[:, :, 0:128], in_=kT[:, :, TT:TT + 128]), reads=[B_kT],
                      writes=[B_kT])
                kb.op("dve", lambda e: e.tensor_copy(out=vtok[:, 0, :, :], in_=vtok[:, NQB, :, :]), reads=[B_vtok],
                      writes=[B_vtok])
                kb.op("dve", lambda e: e.tensor_copy(out=kmx[:, 0, :], in_=kmx[:, NQB, :]), reads=[B_kmx], writes=[B_kmx])
                kb.barrier()
                if not is_main:
                    mute(True)

            ckpt("ATT", [("oat", oat[:, :, :].rearrange("p a b -> p (a b)"), B_oat, BF16)])
            with ExitStack() as sg_:
                sig = sbt(sg_, "sig2", [128, TT], F32)
                B_sig = Buf("sig2")
                tmpm = sbt(sg_, "tmpm", [128, TT], F32)
                B_tmpm = Buf("tmpm")
                brt = sbt(sg_, "brt2", [128, 2, TT], F32)
                B_brt = [Buf("brt20"), Buf("brt21")]
                for u in range(16):
                    wtb, bwb = ws.get(("wab", t, u), wsegs("wab", 0, D, u * 256, 256))
                    if not null:
                        for bi in range(2):
                            acc1, Ba1 = next_acc()
                            dense(acc1, Ba1, [(wtb, bwb, list(range(NB)), bi * 128, lambda k, pr: oat[:, k, :], [B_oat], None)])
                            kb.op("dve", lambda e, acc1=acc1, bi=bi: e.tensor_copy(out=brt[:, bi, :], in_=acc1[:, 0:TT]),
                                  reads=[Ba1], writes=[B_brt[bi]])
                    wtg, bwg = proj_in(GATE0 + u * 256, 256, ("gat", t, u))
                    if null:
                        continue
                    for bi in range(2):
                        j = u * 2 + bi
                        acc2, Ba2 = next_acc()
                        dense(acc2, Ba2, [(wtg, bwg, list(range(NB)), bi * 128, lambda k, pr: hT[:, k, :], [B_hT], None)])
                        kb.op("act", lambda e, acc2=acc2: e.activation(out=sig[:, :], in_=acc2[:, 0:TT], func=AF.Sigmoid),
                              reads=[Ba2], writes=[B_sig])
                        kb.op("dve", lambda e, bi=bi: e.tensor_tensor(out=tmpm[:, :], in0=brt[:, bi, :], in1=sig[:, :],
                                                                      op=ALU.mult), reads=[B_brt[bi], B_sig], writes=[B_tmpm])
                        kb.op("dve", lambda e, j=j: e.tensor_tensor(out=gm[:, j, :], in0=gm[:, j, :], in1=tmpm[:, :],
                                                                    op=ALU.add), reads=[B_tmpm, B_gm], writes=[B_gm])
                kb.barrier()

            ckpt("GM2", [("gm2", gm[:, :, :].rearrange("p a b -> p (a b)"), B_gm, BF16)])
            h2T = hT
            B_h2T = B_hT
            with ExitStack() as so:
                moT = sbt(so, "moT", [128, NB, TT], F32)
                B_mo = Buf("moT")
                sq16 = sbt(so, "sq16", [128, TT], BF16)
                B_sq = Buf("sq16")
                rb = sbt(so, "rb", [128, TT], F32)
                B_rb = Buf("rb")
                xt = sbt(so, "xt2", [128, D], F32)
                B_xt = Buf("xt2")
                stg = xt
                B_stg = B_xt
                ssb, B_ssb = pbank[YB], B_pb[YB]
                for u in range(16):
                    wt, bw = ws.get(("wout", t, u), wsegs("wout", 0, D, u * 256, 256))
                    if null:
                        continue
                    for bi in range(2):
                        j = u * 2 + bi
                        acc, Ba = next_acc()
                        dense(acc, Ba, [(wt, bw, list(range(NB)), bi * 128, lambda k, pr: gm[:, k, :], [B_gm], None)])
                        kb.op("dve", lambda e, acc=acc, j=j: e.tensor_copy(out=moT[:, j, :], in_=acc[:, 0:TT]),
                              reads=[Ba], writes=[B_mo])
                        kb.op("act", lambda e, acc=acc: e.activation(out=sq16[:, :], in_=acc[:, 0:TT], func=AF.Square),
                              reads=[Ba], writes=[B_sq])
                        kb.op("pe", lambda e, j=j: e.matmul(ssb[:, 0:TT], lhsT=ones_b, rhs=sq16[:, :], start=(j == 0),
                                                            stop=(j == NB - 1)), reads=[B_sq, B_const], writes=[B_ssb])
                ckpt("O1", [("moT", moT[:, :, :].rearrange("p a b -> p (a b)"), B_mo, F32)])
                if not null:
                    kb.op("dve", lambda e: e.tensor_scalar(out=rb[:, :], in0=ssb[:, 0:TT], scalar1=1.0 / D, scalar2=1e-6,
                                                           op0=ALU.mult, op1=ALU.add), reads=[B_ssb], writes=[B_rb])
                    kb.op("act", lambda e: e.activation(out=rb[:, :], in_=rb[:, :], func=AF.Sqrt), reads=[B_rb],
                          writes=[B_rb])
                    kb.op("dve", lambda e: e.reciprocal(out=rb[:, :], in_=rb[:, :]), reads=[B_rb], writes=[B_rb])
                    for tb in range(NQB):
                        r0 = tok0 + tb * 128
                        kb.dma("sp", xt[:, :], x[r0:r0 + 128, :], writes=[B_xt])
                        for k4 in range(NB // 4):
                            pb, Bp = next_rot()

                            def fn(e, k4=k4, pb=pb):
                                inst = None
                                for i in range(4):
                                    kc = k4 * 4 + i
                                    inst = e.matmul(pb[:, i * 128:(i + 1) * 128], lhsT=xt[:, kc * 128:(kc + 1) * 128],
                                                    rhs=ident_f, start=True, stop=True)
                                return inst
                            kb.op("pe", fn, reads=[B_xt, B_const], writes=[Bp])
                            for i in range(4):
                                kc = k4 * 4 + i
                                sl = slice(tb * 128, (tb + 1) * 128)
                                kb.op("dve", lambda e, kc=kc, sl=sl: e.scalar_tensor_tensor(
                                    out=moT[:, kc, sl], in0=moT[:, kc, sl], scalar=pcol("nmo", kc), in1=rb[:, sl],
                                    op0=ALU.mult, op1=ALU.mult), reads=[B_mo, B_rb, B_const], writes=[B_mo])
                                kb.op("dve", lambda e, kc=kc, sl=sl, pb=pb, i=i: e.tensor_tensor(
                                    out=moT[:, kc, sl], in0=moT[:, kc, sl], in1=pb[:, i * 128:(i + 1) * 128], op=ALU.add),
                                    reads=[B_mo, Bp], writes=[B_mo])
                    ckpt("O2", [("x1T", moT[:, :, :].rearrange("p a b -> p (a b)"), B_mo, F32)])
                    for j in range(NB):
                        kb.op("act", lambda e, j=j: e.activation(out=sq16[:, :], in_=moT[:, j, :], func=AF.Square),
                              reads=[B_mo], writes=[B_sq])
                        kb.op("pe", lambda e, j=j: e.matmul(ssb[:, 0:TT], lhsT=ones_b, rhs=sq16[:, :], start=(j == 0),
                                                            stop=(j == NB - 1)), reads=[B_sq, B_const], writes=[B_ssb])
                    kb.op("dve", lambda e: e.tensor_scalar(out=rb[:, :], in0=ssb[:, 0:TT], scalar1=1.0 / D, scalar2=1e-6,
                                                           op0=ALU.mult, op1=ALU.add), reads=[B_ssb], writes=[B_rb])
                    kb.op("act", lambda e: e.activation(out=rb[:, :], in_=rb[:, :], func=AF.Sqrt), reads=[B_rb],
                          writes=[B_rb])
                    kb.op("dve", lambda e: e.reciprocal(out=rb[:, :], in_=rb[:, :]), reads=[B_rb], writes=[B_rb])
                    for j in range(NB):
                        kb.op("dve", lambda e, j=j: e.scalar_tensor_tensor(out=h2T[:, j, :], in0=moT[:, j, :],
                                                                           scalar=pcol("nfp", j), in1=rb[:, :],
                                                                           op0=ALU.mult, op1=ALU.mult),
                              reads=[B_mo, B_rb, B_const], writes=[B_h2T])
                    ckpt("O3", [("h2T", hT[:, :, :].rearrange("p a b -> p (a b)"), B_hT, BF16)])
                    for tb in range(NQB):
                        r0 = tok0 + tb * 128
                        for k4 in range(NB // 4):
                            pb, Bp = next_rot()

                            def fn(e, k4=k4, pb=pb, tb=tb):
                                inst = None
                                for i in range(4):
                                    kc = k4 * 4 + i
                                    inst = e.matmul(pb[:, i * 128:(i + 1) * 128], lhsT=moT[:, kc, tb * 128:(tb + 1) * 128],
                                                    rhs=ident_f, start=True, stop=True)
                                return inst
                            kb.op("pe", fn, reads=[B_mo, B_const], writes=[Bp])
                            if k4 % 2:
                                kb.op("act", lambda e, k4=k4, pb=pb: e.activation(out=stg[:, k4 * 512:(k4 + 1) * 512],
                                                                                   in_=pb[:, :], func=AF.Copy),
                                      reads=[Bp], writes=[B_stg])
                            else:
                                kb.op("dve", lambda e, k4=k4, pb=pb: e.tensor_copy(out=stg[:, k4 * 512:(k4 + 1) * 512],
                                                                                    in_=pb[:, :]), reads=[Bp], writes=[B_stg])
                        kb.dma("sp", out[otok0 + tb * 128:otok0 + tb * 128 + 128, :], stg[:, :], reads=[B_stg])
                kb.barrier()

        ckpt("O", [("h2T", hT[:, :, :].rearrange("p a b -> p (a b)"), B_hT, BF16)])
        with ExitStack() as sf:
            aT = sbt(sf, "aT", [128, NFB, TT], BF16)
            B_aT = Buf("aT")
            with ExitStack() as sf1:
                sgl = sbt(sf1, "sgl", [128, 2, TT], F32)
                B_sgl = [Buf("sgl0"), Buf("sgl1")]
                for u in range(FFN // 256):
                    wtg, bwg = ws.get(("ffg", t, u), wsegs("wg", 0, D, u * 256, 256))
                    if not null:
                        for bi in range(2):
                            acc1, Ba1 = next_acc()
                            dense(acc1, Ba1, [(wtg, bwg, list(range(NB)), bi * 128, lambda k, pr: h2T[:, k, :], [B_h2T], None)])
                            kb.op("act", lambda e, acc1=acc1, bi=bi: e.activation(out=sgl[:, bi, :], in_=acc1[:, 0:TT],
                                                                                  func=AF.Silu),
                                  reads=[Ba1], writes=[B_sgl[bi]])
                    wtu, bwu = ws.get(("ffu", t, u), wsegs("wu", 0, D, u * 256, 256))
                    if null:
                        continue
                    for bi in range(2):
                        j = u * 2 + bi
                        acc2, Ba2 = next_acc()
                        dense(acc2, Ba2, [(wtu, bwu, list(range(NB)), bi * 128, lambda k, pr: h2T[:, k, :], [B_h2T], None)])
                        kb.op("dve", lambda e, acc2=acc2, j=j, bi=bi: e.tensor_tensor(out=aT[:, j, :], in0=acc2[:, 0:TT],
                                                                                      in1=sgl[:, bi, :], op=ALU.mult),
                              reads=[Ba2, B_sgl[bi]], writes=[B_aT])
                kb.barrier()
            with ExitStack() as sf2:
                fT = sbt(sf2, "fT", [128, NB, TT], F32)
                B_fT = Buf("fT")
                sq16 = sbt(sf2, "sq16b", [128, TT], BF16)
                B_sq = Buf("sq16b")
                rb = sbt(sf2, "rb2", [128, TT], F32)
                B_rb = Buf("rb2")
                rtok = sbt(sf2, "rtok", [128, NQB], F32)
                B_rtok = Buf("rtok")
                x1c = sbt(sf2, "x1c", [128, 1024], F32)
                B_x1c = Buf("x1c")
                oc = sbt(sf2, "oc", [128, 1024], F32)
                B_oc = Buf("oc")
                ssb, B_ssb = pbank[YB], B_pb[YB]
                kparts = [(0, 32), (32, 32), (64, NFB - 64)]
                for u in range(16):
                    accs = [next_acc(), next_acc()]
                    for pi, (k0, nk_) in enumerate(kparts):
                        wt, bw = ws.get(("ffd", t, u, pi), wsegs("wd", k0 * 128, nk_ * 128, u * 256, 256))
                        if null:
                            continue
                        for bi in range(2):
                            acc, Ba = accs[bi]

                            def fnd(e, acc=acc, wt=wt, bi=bi, k0=k0, nk_=nk_, pi=pi):
                                inst = None
                                for k in range(nk_):
                                    inst = e.matmul(acc[:, 0:TT], lhsT=wt[:, k, bi * 128:(bi + 1) * 128], rhs=aT[:, k0 + k, :],
                                                    start=(pi == 0 and k == 0), stop=(pi == 2 and k == nk_ - 1))
                                return inst
                            kb.op("pe", fnd, reads=[bw, B_aT], writes=[Ba])
                    if null:
                        continue
                    for bi in range(2):
                        j = u * 2 + bi
                        acc, Ba = accs[bi]
                        kb.op("dve", lambda e, acc=acc, j=j: e.tensor_scalar_mul(out=fT[:, j, :], in0=acc[:, 0:TT],
                                                                                 scalar1=pcol("nfo", j)),
                              reads=[Ba, B_const], writes=[B_fT])
                        kb.op("act", lambda e, acc=acc: e.activation(out=sq16[:, :], in_=acc[:, 0:TT], func=AF.Square),
                              reads=[Ba], writes=[B_sq])
                        kb.op("pe", lambda e, j=j: e.matmul(ssb[:, 0:TT], lhsT=ones_b, rhs=sq16[:, :], start=(j == 0),
                                                            stop=(j == NB - 1)), reads=[B_sq, B_const], writes=[B_ssb])
                if not null:
                    kb.op("dve", lambda e: e.tensor_scalar(out=rb[:, :], in0=ssb[:, 0:TT], scalar1=1.0 / D, scalar2=1e-6,
                                                           op0=ALU.mult, op1=ALU.add), reads=[B_ssb], writes=[B_rb])
                    kb.op("act", lambda e: e.activation(out=rb[:, :], in_=rb[:, :], func=AF.Sqrt), reads=[B_rb],
                          writes=[B_rb])
                    kb.op("dve", lambda e: e.reciprocal(out=rb[:, :], in_=rb[:, :]), reads=[B_rb], writes=[B_rb])
                    for j in range(NB):
                        kb.op("dve", lambda e, j=j: e.tensor_tensor(out=fT[:, j, :], in0=fT[:, j, :], in1=rb[:, :],
                                                                    op=ALU.mult), reads=[B_fT, B_rb], writes=[B_fT])
                    for tb in range(NQB):
                        r0 = tok0 + tb * 128
                        for q4 in range(4):
                            kb.dma("sp", x1c[:, :], out[otok0 + tb * 128:otok0 + tb * 128 + 128, q4 * 1024:(q4 + 1) * 1024],
                                   writes=[B_x1c])
                            for h2 in range(2):
                                pb, Bp = next_rot()

                                def fn(e, pb=pb, q4=q4, h2=h2, tb=tb):
                                    inst = None
                                    for i in range(4):
                                        kc = q4 * 8 + h2 * 4 + i
                                        inst = e.matmul(pb[:, i * 128:(i + 1) * 128],
                                                        lhsT=fT[:, kc, tb * 128:(tb + 1) * 128], rhs=ident_f, start=True,
                                                        stop=True)
                                    return inst
                                kb.op("pe", fn, reads=[B_fT, B_const], writes=[Bp])
                                kb.op("dve", lambda e, pb=pb, h2=h2: e.tensor_tensor(
                                    out=oc[:, h2 * 512:(h2 + 1) * 512], in0=pb[:, :], in1=x1c[:, h2 * 512:(h2 + 1) * 512],
                                    op=ALU.add), reads=[Bp, B_x1c], writes=[B_oc])
                            kb.dma("sp", out[otok0 + tb * 128:otok0 + tb * 128 + 128, q4 * 1024:(q4 + 1) * 1024], oc[:, :],
                                   reads=[B_oc])
                kb.barrier()
        stile.close()
    mute(False)
    kb.finish()


def _dram(nc, NPRE, NT):
    def din(name, shape, dt=F32):
        return nc.dram_tensor(name, shape, dt, kind="ExternalInput").ap()
    NA = NPRE + NT
    T_ = {
        "x": din("x", [NA * TT, D]),
        "w_in": din("w_in", [D, IN_DIM]),
        "w2": din("w2", [128, D]), "a2": din("a2", [128, D]), "g2": din("g2", [480, D]),
        "wab": din("wab", [D, D]), "wrb": din("wrb", [D, D]), "wout": din("wout", [D, D]),
        "wg": din("wg", [D, FFN]), "wu": din("wu", [D, FFN]), "wd": din("wd", [FFN, D]),
        "pvec": din("pvec", [128, NPV]), "cbf": din("cbf", [128, NCBF], BF16), "cf32": din("cf32", [128, NCF]),
        "ropec": din("ropec", [128, NA * TT]), "ropes": din("ropes", [128, NA * TT]),
    }
    T_["out"] = nc.dram_tensor("out", [NT * TT, D], F32, kind="ExternalOutput").ap()
    if NPRE > 0:
        for i_ in range(3):
            T_["wsc%d" % i_] = nc.dram_tensor("wsc%d" % i_, [NUSCR // 3, 128, 32 * 256], BF16, kind="Internal").ap()
    return T_


def build(NPRE, NT):
    ws = WStream()
    nc0 = bass.Bass("TRN2", target_bir_lowering=False)
    with ExitStack() as es0:
        emit_program(nc0, KB(nc0, es0, null=True), ws, es0, _dram(nc0, NPRE, NT), NPRE, NT)
    nc = bass.Bass("TRN2", target_bir_lowering=False)
    with ExitStack() as es:
        kb = KB(nc, es)
        emit_program(nc, kb, ws, es, _dram(nc, NPRE, NT), NPRE, NT)
    return nc


def _colmajor(v, nblk):
    v = np.asarray(v, np.float32).reshape(-1)
    o = np.zeros(nblk * 128, np.float32)
    o[:v.size] = v
    return o.reshape(nblk, 128).T


def _host_consts(pos, first_has_prev):
    cbf = np.zeros((128, NCBF), np.float32)
    cbf[:, CB_ID:CB_ID + 128] = np.eye(128)
    cbf[:, CB_ONES:CB_ONES + 128] = 1.0
    p = np.arange(128)
    cbf[:, CB_BDONES:CB_BDONES + 128] = (p[:, None] // 64 == p[None, :] // 64)
    cbf[:, CB_PSW:CB_PSW + 128] = (p[:, None] == (p[None, :] + 64) % 128)
    mprev = np.where(p[:, None] > p[None, :], 0.0, MASKNEG)
    mcur = np.where(p[:, None] <= p[None, :], 0.0, MASKNEG)
    cbf[:, CB_MPREV:CB_MPREV + 512] = np.tile(mprev, (1, 4))
    cbf[:, CB_MCUR:CB_MCUR + 512] = np.tile(mcur, (1, 4))
    cbf[:, CB_MPREV0:CB_MPREV0 + 512] = np.tile(mprev, (1, 4)) if first_has_prev else MASKNEG
    same = (p[:, None] // 64 == p[None, :] // 64)
    lt = same & (p[:, None] % 64 < p[None, :] % 64)
    gt = same & (p[:, None] % 64 > p[None, :] % 64)
    incl = (p[:, None] % 64 <= np.arange(64)[None, :])
    cbf[:, CB_MCAT:CB_MCAT + 512] = np.concatenate([lt, lt, gt, incl, incl], axis=1)
    bdm = np.zeros((128, NCH, 2, 64), np.float32)
    bdm[:64, :, 0, :] = 1.0
    bdm[64:, :, 1, :] = 1.0
    cbf[:, CB_BDM:CB_BDM + NCH * 128] = bdm.reshape(128, -1)
    cf = np.zeros((128, NCF), np.float32)
    cf[:, CF_ID:CF_ID + 128] = np.eye(128)
    seg = np.ones(TT, np.float32)
    seg[::64] = 0.0
    cf[:, CF_SEG:CF_SEG + TT] = seg[None, :]
    pos = np.asarray(pos, np.float32)
    inv = (np.float32(10000.0) ** (-np.arange(0, 128, 2, dtype=np.float32) / np.float32(128))).astype(np.float32)
    ang = (pos[:, None] * inv[None, :]).astype(np.float32)
    c = np.cos(ang).astype(np.float32).T
    s_ = np.sin(ang).astype(np.float32).T
    ropec = np.concatenate([c, c], axis=0)
    ropes = np.concatenate([-s_, s_], axis=0)
    return cbf.astype(ml_dtypes.bfloat16), cf, np.ascontiguousarray(ropec), np.ascontiguousarray(ropes)


def _pvec(inp):
    pv = np.zeros((128, NPV), np.float32)

    def put(name, v, n):
        pv[:, PV[name]:PV[name] + n] = _colmajor(v, n)
    put("nmp", inp["norm_mix_pre"][0], 32)
    put("nmo", inp["norm_mix_post"][0], 32)
    put("nfp", inp["norm_ffn_pre"][0], 32)
    put("nfo", inp["norm_ffn_post"][0], 32)
    put("bqkv", inp["b_qkv"][0], 48)
    put("mu", inp["mu_shift"][0], 102)
    put("w0", inp["w0"][0], 32)
    put("a0", inp["a0"][0], 32)
    put("kk", inp["k_k"][0], 32)
    put("ka", inp["k_a"][0], 32)
    put("lnw", inp["ln_x_w"][0], 32)
    put("lnb", inp["ln_x_b"][0], 32)
    put("rk", inp["r_k"][0], 32)
    pv[:, PV["sink"]:PV["sink"] + 32] = np.asarray(inp["att_sinks"][0], np.float32)[None, :]
    return pv


def run(inp, seqs, nsplit, trace=False):
    S = seqs[0].shape[0]
    NT = S // nsplit // TT
    NPRE = (nsplit - 1) * NT
    nc = build(NPRE, NT)
    pv = _pvec(inp)
    f = lambda k: np.ascontiguousarray(np.asarray(inp[k], np.float32)[0])
    common = {
        "w_in": f("w_in"), "w2": f("w2"), "a2": f("a2"), "g2": f("g2"),
        "wab": f("w_att_branch"), "wrb": f("w_rwkv_branch"), "wout": f("w_out"),
        "wg": f("w_ffn_gate"), "wu": f("w_ffn_up"), "wd": f("w_ffn_down"), "pvec": pv,
    }
    in_maps = []
    for xseq in seqs:
        xseq = np.asarray(xseq, np.float32)
        for h in range(nsplit):
            lo = h * NT * TT
            npad = (NPRE * TT) - lo
            xc = np.zeros(((NPRE + NT) * TT, D), np.float32)
            xc[npad:] = xseq[0:lo + NT * TT]
            pos = np.concatenate([np.zeros(npad, np.float32), np.arange(lo + NT * TT, dtype=np.float32)])
            cbf, cf, ropec, ropes = _host_consts(pos, first_has_prev=(h > 0))
            m = dict(common)
            m.update({"x": xc, "cbf": cbf, "cf32": cf, "ropec": ropec, "ropes": ropes})
            in_maps.append(m)
    res = run_bass_kernel_spmd(nc, in_maps, core_ids=list(range(len(in_maps))), trace=trace)
    DBG["res"] = res.results
    outs = []
    for b in range(len(seqs)):
        outs.append(np.concatenate([res.results[b * nsplit + h]["out"] for h in range(nsplit)], axis=0))
    return outs, res


def kernel(**inputs):
    x = np.asarray(inputs["x"], np.float32)
    B = x.shape[0]
    outs, _ = run(inputs, [x[b] for b in range(B)], 2)
    return np.stack(outs, axis=0).astype(np.float32)
```
